# Optimizing a Trainium2 kernel written in Bass

```python
import jax, jax.numpy as jnp
from jax import lax
import numpy as np

D_MODEL = 2048
BATCH = 4
SEQ = 8192
DEPTH = 4

F32 = jnp.float32
NORM_EPS = 1e-6
HALF = D_MODEL // 2
DN_DK = 128
DN_DV = 128
DN_HEADS = HALF // DN_DV
DN_QK = DN_HEADS * DN_DK
DN_V = DN_HEADS * DN_DV
DN_CONV = 4
DN_CHUNK = 64
LRU_WIDTH = HALF
LRU_BLOCKS = 8
LRU_BLK = LRU_WIDTH // LRU_BLOCKS
LRU_CONV = 4
LRU_C = 8.0
SWA_DIM = 128
SWA_HEADS = HALF // SWA_DIM
SWA_W = SWA_HEADS * SWA_DIM
SWA_BRANCHES = ((128, 1), (512, 4), (2048, 16))
SWA_BLOCK = 128
RET_DK = 128
RET_DV = 256
RET_HEADS = HALF // RET_DV
RET_QK = RET_HEADS * RET_DK
RET_V = RET_HEADS * RET_DV
RET_CHUNK = 64
GN_EPS = 1e-5
D_FF = 4 * D_MODEL
PLE_DIM = 256
N_EVEN = (DEPTH + 1) // 2
N_ODD = DEPTH // 2
EV_SPLITS = (2 * DN_QK + DN_V, DN_V, DN_HEADS, DN_HEADS, LRU_WIDTH, LRU_WIDTH)
EV_IN = sum(EV_SPLITS)
EV_OUT = DN_V + LRU_WIDTH
OD_SPLITS = (SWA_W, SWA_W, SWA_W, RET_QK, RET_QK, RET_V, RET_V)
OD_IN = sum(OD_SPLITS)
OD_OUT = SWA_W + RET_V

kernel_name = "hybrid_deltanet_rglru_dilated_retention_trunk"


def rms_norm(x, w, eps=NORM_EPS):
    xf = x.astype(F32)
    y = xf * lax.rsqrt(jnp.mean(xf * xf, axis=-1, keepdims=True) + eps)
    return (y * w.astype(F32)).astype(x.dtype)


def l2_normalize(x, eps=1e-6):
    return x * lax.rsqrt(jnp.sum(x * x, axis=-1, keepdims=True) + eps)


def head_group_norm(x, eps=GN_EPS):
    mu = jnp.mean(x, axis=-1, keepdims=True)
    xc = x - mu
    return xc * lax.rsqrt(jnp.mean(xc * xc, axis=-1, keepdims=True) + eps)


def split_cols(x, sizes):
    return jnp.split(x, np.cumsum(sizes)[:-1].tolist(), axis=-1)


def to_heads(x, n_heads):
    b, t, _ = x.shape
    return x.reshape(b, t, n_heads, -1).transpose(0, 2, 1, 3)


def merge_heads(x):
    b, h, t, d = x.shape
    return x.transpose(0, 2, 1, 3).reshape(b, t, h * d)


def to_chunks(x, c):
    b, h, t = x.shape[:3]
    return jnp.moveaxis(x.reshape(b, h, t // c, c, *x.shape[3:]), 2, 0)


def from_chunks(x):
    n, b, h, c, d = x.shape
    return jnp.moveaxis(x, 0, 2).reshape(b, h, n * c, d)


def causal_dwconv(x, w):
    width, t = w.shape[0], x.shape[1]
    xp = jnp.pad(x, ((0, 0), (width - 1, 0), (0, 0)))
    y = xp[:, 0:t] * w[0]
    for j in range(1, width):
        y = y + xp[:, j:j + t] * w[j]
    return y


def gated_delta_rule(q, k, v, g, beta):
    b, h, _, dk = q.shape
    dv = v.shape[-1]
    c = DN_CHUNK
    q, k, v = to_chunks(q, c), to_chunks(k, c), to_chunks(v, c)
    g, beta = to_chunks(g, c), to_chunks(beta, c)
    gc = jnp.cumsum(g, axis=-1)
    causal = jnp.tril(jnp.ones((c, c), bool))
    strict = jnp.tril(jnp.ones((c, c), bool), -1)
    decay = jnp.exp(jnp.where(causal, gc[..., :, None] - gc[..., None, :], -jnp.inf))
    kb = k * beta[..., None]
    m = jnp.where(strict, jnp.einsum('nbhid,nbhjd->nbhij', kb, k) * decay, 0.0)
    rhs = jnp.concatenate([v * beta[..., None], kb * jnp.exp(gc)[..., None]], axis=-1)
    sol = lax.linalg.triangular_solve(jnp.eye(c, dtype=F32) + m, rhs, left_side=True, lower=True, unit_diagonal=True)
    u, w = sol[..., :dv], sol[..., dv:]
    qk = jnp.einsum('nbhid,nbhjd->nbhij', q, k) * decay
    q_dec = q * jnp.exp(gc)[..., None]
    k_dec = k * jnp.exp(gc[..., -1:] - gc)[..., None]
    g_tot = jnp.exp(gc[..., -1])[..., None, None]

    def step(state, xs):
        u_n, w_n, qk_n, qd_n, kd_n, gt_n = xs
        v_new = u_n - jnp.einsum('bhcd,bhde->bhce', w_n, state)
        o = jnp.einsum('bhcd,bhde->bhce', qd_n, state) + jnp.einsum('bhij,bhje->bhie', qk_n, v_new)
        state = state * gt_n + jnp.einsum('bhcd,bhce->bhde', kd_n, v_new)
        return state, o

    _, o = lax.scan(step, jnp.zeros((b, h, dk, dv), F32), (u, w, qk, q_dec, k_dec, g_tot))
    return from_chunks(o)


def rg_lru_branch(xr, yr, conv_w, conv_b, wa, ba, wx, bx, lam):
    b, t, _ = xr.shape
    xc = (causal_dwconv(xr, conv_w) + conv_b).astype(F32)
    xb = xc.reshape(b, t, LRU_BLOCKS, LRU_BLK)
    r = jax.nn.sigmoid(jnp.einsum('btgi,gij->btgj', xb, wa.astype(F32)).reshape(b, t, -1) + ba.astype(F32))
    i = jax.nn.sigmoid(jnp.einsum('btgi,gij->btgj', xb, wx.astype(F32)).reshape(b, t, -1) + bx.astype(F32))
    log_a = -LRU_C * r * jax.nn.softplus(-lam.astype(F32))
    a = jnp.exp(log_a)
    u = jnp.sqrt(-jnp.expm1(2.0 * log_a)) * (i * xc)

    def combine(e1, e2):
        a1, b1 = e1
        a2, b2 = e2
        return a1 * a2, a2 * b1 + b2

    _, hs = lax.associative_scan(combine, (a, u), axis=1)
    return hs * jax.nn.gelu(yr.astype(F32))


def even_mixer(hn, w_in, w_out, dn_conv_w, dn_a_log, dn_dt_bias, dn_norm_w,
               lru_conv_w, lru_conv_b, lru_wa, lru_ba, lru_wx, lru_bx, lru_lambda):
    proj = hn @ w_in
    qkv, z, b_raw, a_raw, xr, yr = split_cols(proj, EV_SPLITS)
    qkv = jax.nn.silu(causal_dwconv(qkv, dn_conv_w)).astype(F32)
    q, k, v = split_cols(qkv, (DN_QK, DN_QK, DN_V))
    q = l2_normalize(to_heads(q, DN_HEADS)) * (DN_DK ** -0.5)
    k = l2_normalize(to_heads(k, DN_HEADS))
    v = to_heads(v, DN_HEADS)
    beta = jax.nn.sigmoid(b_raw.astype(F32)).transpose(0, 2, 1)
    g = (-jnp.exp(dn_a_log.astype(F32)) * jax.nn.softplus(a_raw.astype(F32) + dn_dt_bias.astype(F32))).transpose(0, 2, 1)
    o = gated_delta_rule(q, k, v, g, beta)
    o = rms_norm(o, dn_norm_w) * jax.nn.silu(to_heads(z.astype(F32), DN_HEADS))
    y_a = merge_heads(o)
    y_b = rg_lru_branch(xr, yr, lru_conv_w, lru_conv_b, lru_wa, lru_ba, lru_wx, lru_bx, lru_lambda)
    return jnp.concatenate([y_a, y_b], axis=-1).astype(hn.dtype) @ w_out


def dilated_branch(q, k, v, slopes, window, dilation):
    b, h, t, dh = q.shape
    d = dilation
    span = window // dilation
    n_len = t // d
    nb = -(-n_len // SWA_BLOCK)
    lp = nb * SWA_BLOCK

    def to_res(x):
        x = x.reshape(b, h, n_len, d, dh).transpose(0, 1, 3, 2, 4)
        x = jnp.pad(x, ((0, 0), (0, 0), (0, 0), (0, lp - n_len), (0, 0)))
        return x.reshape(b, h, d, nb, SWA_BLOCK, dh)

    def with_prev(x):
        prev = jnp.pad(x[:, :, :, :-1], ((0, 0), (0, 0), (0, 0), (1, 0), (0, 0), (0, 0)))
        return jnp.concatenate([prev, x], axis=4)

    def from_res(x):
        x = x.reshape(b, h, d, lp, *x.shape[5:])[:, :, :, :n_len]
        x = jnp.moveaxis(x, 2, 3)
        return x.reshape(b, h, t, *x.shape[4:])

    qr = to_res(q)
    kk, vv = with_prev(to_res(k)), with_prev(to_res(v))
    iq = jnp.arange(SWA_BLOCK)
    ik = jnp.arange(2 * SWA_BLOCK)
    rel = SWA_BLOCK + iq[:, None] - ik[None, :]
    blk = jnp.arange(nb)
    valid = (rel >= 0) & (rel <= span) & ((blk[:, None, None] > 0) | (ik >= SWA_BLOCK)[None, None, :])
    s = jnp.einsum('bhrnqd,bhrnkd->bhrnqk', qr, kk)
    s = s - slopes[:, None, None, None, None] * (rel * d).astype(F32)
    s = jnp.where(valid, s, -jnp.inf)
    mx = jnp.max(s, axis=-1, keepdims=True)
    pr = jnp.exp(s - mx)
    den = jnp.sum(pr, axis=-1, keepdims=True)
    o = jnp.einsum('bhrnqk,bhrnkd->bhrnqd', pr, vv) / den
    lse = (mx + jnp.log(den))[..., 0]
    return from_res(o), from_res(lse)


def retention(q, k, v, log_gamma):
    b, h, _, dk = q.shape
    dv = v.shape[-1]
    c = RET_CHUNK
    idx = jnp.arange(c, dtype=F32)
    rel = idx[:, None] - idx[None, :]
    dmask = jnp.where(rel >= 0, jnp.exp(jnp.maximum(rel, 0.0)[None] * log_gamma[:, None, None]), 0.0)
    qc, kc, vc = to_chunks(q, c), to_chunks(k, c), to_chunks(v, c)
    intra = jnp.einsum('nbhij,nbhje->nbhie', jnp.einsum('nbhid,nbhjd->nbhij', qc, kc) * dmask, vc)
    q_dec = qc * jnp.exp((idx + 1.0)[None, :] * log_gamma[:, None])[:, :, None]
    k_dec = kc * jnp.exp((c - 1.0 - idx)[None, :] * log_gamma[:, None])[:, :, None]
    chunk_decay = jnp.exp(c * log_gamma)[:, None, None]

    def step(state, xs):
        qd, kd, vn = xs
        o = jnp.einsum('bhcd,bhde->bhce', qd, state)
        state = state * chunk_decay + jnp.einsum('bhcd,bhce->bhde', kd, vn)
        return state, o

    _, inter = lax.scan(step, jnp.zeros((b, h, dk, dv), F32), (q_dec, k_dec, vc))
    return from_chunks(intra + inter)


def odd_mixer(hn, w_in, w_out):
    proj = hn @ w_in
    cq, ck, cv, rq, rk, rv, rg = split_cols(proj.astype(F32), OD_SPLITS)
    q = to_heads(cq, SWA_HEADS) * (SWA_DIM ** -0.5)
    k = to_heads(ck, SWA_HEADS)
    v = to_heads(cv, SWA_HEADS)
    slopes = jnp.exp2(-8.0 * jnp.arange(1, SWA_HEADS + 1, dtype=F32) / SWA_HEADS)
    outs, lses = [], []
    for window, dilation in SWA_BRANCHES:
        o_i, lse_i = dilated_branch(q, k, v, slopes, window, dilation)
        outs.append(o_i)
        lses.append(lse_i)
    wts = jax.nn.softmax(jnp.stack(lses), axis=0)
    y_c = merge_heads(jnp.sum(jnp.stack(outs) * wts[..., None], axis=0))
    log_gamma = jnp.log1p(-jnp.exp2(-5.0 - jnp.arange(RET_HEADS, dtype=F32)))
    o_r = retention(to_heads(rq, RET_HEADS), to_heads(rk, RET_HEADS) * (RET_DK ** -0.5), to_heads(rv, RET_HEADS), log_gamma)
    o_r = head_group_norm(o_r) * jax.nn.silu(to_heads(rg, RET_HEADS))
    y_d = merge_heads(o_r)
    return jnp.concatenate([y_c, y_d], axis=-1).astype(hn.dtype) @ w_out


def setup_inputs(seed: int = 0) -> dict:
    key = jax.random.key(seed)
    ks = iter(jax.random.split(key, 32))

    def normal(shape, fan_in):
        return jax.random.normal(next(ks), shape, F32) * (fan_in ** -0.5)

    def gain(shape):
        return 1.0 + 0.02 * jax.random.normal(next(ks), shape, F32)

    def small(shape):
        return 0.02 * jax.random.normal(next(ks), shape, F32)

    x = jax.random.normal(next(ks), (BATCH, SEQ, D_MODEL), F32)
    p = jax.random.normal(next(ks), (DEPTH, BATCH, SEQ, PLE_DIM), F32)
    ln_mix_w = gain((DEPTH, D_MODEL))
    ln_mlp_w = gain((DEPTH, D_MODEL))
    ln_ple_w = gain((DEPTH, D_MODEL))
    w_up = normal((DEPTH, D_MODEL, D_FF), D_MODEL)
    w_down = normal((DEPTH, D_FF, D_MODEL), D_FF)
    w_ple_proj = normal((DEPTH, PLE_DIM, D_MODEL), PLE_DIM)
    w_ple_gate = normal((DEPTH, D_MODEL, D_MODEL), D_MODEL)
    ln_final_w = gain((D_MODEL,))
    ev_w_in = normal((N_EVEN, D_MODEL, EV_IN), D_MODEL)
    ev_w_out = normal((N_EVEN, EV_OUT, D_MODEL), EV_OUT)
    dn_conv_w = normal((N_EVEN, DN_CONV, 2 * DN_QK + DN_V), DN_CONV)
    dn_a_log = jnp.log(jax.random.uniform(next(ks), (N_EVEN, DN_HEADS), F32, 1.0, 16.0))
    dt = jnp.exp(jax.random.uniform(next(ks), (N_EVEN, DN_HEADS), F32, float(np.log(1e-3)), float(np.log(1e-1))))
    dn_dt_bias = dt + jnp.log(-jnp.expm1(-dt))
    dn_norm_w = gain((N_EVEN, DN_DV))
    lru_conv_w = normal((N_EVEN, LRU_CONV, LRU_WIDTH), LRU_CONV)
    lru_conv_b = small((N_EVEN, LRU_WIDTH))
    lru_wa = normal((N_EVEN, LRU_BLOCKS, LRU_BLK, LRU_BLK), LRU_BLK)
    lru_ba = small((N_EVEN, LRU_WIDTH))
    lru_wx = normal((N_EVEN, LRU_BLOCKS, LRU_BLK, LRU_BLK), LRU_BLK)
    lru_bx = small((N_EVEN, LRU_WIDTH))
    a_pow_c = jax.random.uniform(next(ks), (N_EVEN, LRU_WIDTH), F32, 0.9, 0.999)
    log_a = jnp.log(a_pow_c) / LRU_C
    lru_lambda = log_a - jnp.log(-jnp.expm1(log_a))
    od_w_in = normal((N_ODD, D_MODEL, OD_IN), D_MODEL)
    od_w_out = normal((N_ODD, OD_OUT, D_MODEL), OD_OUT)
    return {"x": x, "p": p, "ln_mix_w": ln_mix_w, "ln_mlp_w": ln_mlp_w, "ln_ple_w": ln_ple_w,
            "w_up": w_up, "w_down": w_down, "w_ple_proj": w_ple_proj, "w_ple_gate": w_ple_gate,
            "ln_final_w": ln_final_w, "ev_w_in": ev_w_in, "ev_w_out": ev_w_out, "dn_conv_w": dn_conv_w,
            "dn_a_log": dn_a_log, "dn_dt_bias": dn_dt_bias, "dn_norm_w": dn_norm_w,
            "lru_conv_w": lru_conv_w, "lru_conv_b": lru_conv_b, "lru_wa": lru_wa, "lru_ba": lru_ba,
            "lru_wx": lru_wx, "lru_bx": lru_bx, "lru_lambda": lru_lambda,
            "od_w_in": od_w_in, "od_w_out": od_w_out}


def reference(x, p, ln_mix_w, ln_mlp_w, ln_ple_w, w_up, w_down, w_ple_proj, w_ple_gate, ln_final_w,
              ev_w_in, ev_w_out, dn_conv_w, dn_a_log, dn_dt_bias, dn_norm_w,
              lru_conv_w, lru_conv_b, lru_wa, lru_ba, lru_wx, lru_bx, lru_lambda,
              od_w_in, od_w_out):
    h = x
    for i in range(DEPTH):
        j = i // 2
        hn = rms_norm(h, ln_mix_w[i])
        if i % 2 == 0:
            mix = even_mixer(hn, ev_w_in[j], ev_w_out[j], dn_conv_w[j], dn_a_log[j], dn_dt_bias[j], dn_norm_w[j],
                             lru_conv_w[j], lru_conv_b[j], lru_wa[j], lru_ba[j], lru_wx[j], lru_bx[j], lru_lambda[j])
        else:
            mix = odd_mixer(hn, od_w_in[j], od_w_out[j])
        h = h + mix
        hn = rms_norm(h, ln_mlp_w[i])
        h = h + jnp.square(jax.nn.relu(hn @ w_up[i])) @ w_down[i]
        hn = rms_norm(h, ln_ple_w[i])
        h = h + jax.nn.sigmoid(hn @ w_ple_gate[i]) * (p[i] @ w_ple_proj[i])
    return rms_norm(h, ln_final_w)
```

```python
import numpy as np
import concourse.bass as bass
import concourse.mybir as mybir
from concourse.bass_utils import run_bass_kernel_spmd
from contextlib import ExitStack

F32 = mybir.dt.float32
BF16 = mybir.dt.bfloat16
AF = mybir.ActivationFunctionType
ALU = mybir.AluOpType

D_MODEL = 2048
KC = 16
D_FF = 8192
NORM_EPS = 1e-6
ENGS = ['pe', 'act', 'dve', 'pool', 'sp']


class Dep:
    __slots__ = ('w', 'r', 'excl')

    def __init__(self, excl=False):
        self.w = None
        self.r = {}
        self.excl = excl


class Op:
    __slots__ = ('eng', 'fn', 'waits', 'signal', 'dma', 'slot', 'val', 'sigidx', 'prev_slot_op', 'uid')


class Buf:
    def __init__(self, t, d=None):
        self.t = t
        self.d = d if d is not None else Dep()

    def __getitem__(self, idx):
        return self.t[idx]


class Pool:
    def __init__(self, bufs):
        self.bufs = bufs
        self.i = 0

    def next(self):
        b = self.bufs[self.i % len(self.bufs)]
        self.i += 1
        return b


def _dep(d):
    return d.d if isinstance(d, Buf) else d


class Prog:
    NSLOT = 8

    def __init__(self, nc):
        self.nc = nc
        self.es = ExitStack()
        self.ops = {e: [] for e in ENGS}
        self.nops = 0
        self.dma_cnt = {e: 0 for e in ENGS}
        self.slot_last = {}
        self.slot_count = {}
        self.scope = self.es
        self.phase_id = 0
        self.last_real = {}

    def begin_phase(self):
        self.phase_id += 1
        self.scope = ExitStack()

    def end_phase(self):
        lasts = dict(self.last_real)
        slots = dict(self.slot_last)
        for e in ENGS:
            o = self.op(e, None)
            for e2, p in lasts.items():
                if e2 != e:
                    p.signal = True
                    o.waits.append(p)
            for p in slots.values():
                o.waits.append(p)
        self.scope.close()
        self.scope = self.es

    def sb(self, name, shape, dtype=F32):
        name = "%s_p%d" % (name, self.phase_id)
        return Buf(self.scope.enter_context(self.nc.sbuf_tensor(name, list(shape), dtype)))

    def sbpool(self, name, shape, dtype, n):
        return Pool([self.sb("%s_%d" % (name, i), shape, dtype) for i in range(n)])

    def ps(self, name, shape, dtype=F32):
        name = "%s_p%d" % (name, self.phase_id)
        return Buf(self.scope.enter_context(self.nc.psum_tensor(name, list(shape), dtype)), Dep(excl=True))

    def dram(self, name, shape, dtype=F32, kind="Internal"):
        return Buf(self.nc.dram_tensor(name, list(shape), dtype, kind=kind))

    def op(self, eng, fn, reads=(), writes=(), dma=False):
        o = Op()
        o.eng, o.fn, o.waits, o.signal, o.dma = eng, fn, [], False, dma
        o.slot = o.val = o.sigidx = o.prev_slot_op = None
        o.uid = self.nops
        self.nops += 1
        seen = set()
        rd = [_dep(d) for d in reads]
        wr = [_dep(d) for d in writes]
        wr = wr + [d for d in rd if d.excl]
        rd = [d for d in rd if not d.excl]

        def need(p):
            if p is None or p is o or id(p) in seen:
                return
            seen.add(id(p))
            p.signal = True
            o.waits.append(p)

        def same(p):
            return p.eng == eng and not p.dma and not dma

        for d in rd:
            p = d.w
            if p is not None and not (same(p) and eng == 'pe'):
                need(p)
        for d in wr:
            p = d.w
            if p is not None and not (same(p) and not d.excl):
                if not (same(p) and eng == 'pe'):
                    need(p)
            for q in d.r.values():
                if not same(q):
                    need(q)
        for d in rd:
            d.r[eng if not dma else ('dma', o.uid)] = o
        for d in wr:
            d.w = o
            d.r = {}
        if dma:
            k = self.dma_cnt[eng]
            self.dma_cnt[eng] += 1
            o.slot = (eng, k % self.NSLOT)
            o.prev_slot_op = self.slot_last.get(o.slot)
            self.slot_count[o.slot] = self.slot_count.get(o.slot, 0) + 1
            o.val = 16 * self.slot_count[o.slot]
            self.slot_last[o.slot] = o
            o.signal = True
        self.ops[eng].append(o)
        if fn is not None and not dma:
            self.last_real[eng] = o
        return o

    def dma(self, eng, out, in_, reads=(), writes=()):
        return self.op(eng, lambda e: e.dma_start(out=out, in_=in_), reads, writes, dma=True)

    def mm(self, out, lhsT, rhs, start, stop, reads, writes):
        return self.op('pe', lambda e: e.matmul(out, lhsT=lhsT, rhs=rhs, start=start, stop=stop), reads, writes)

    def tr(self, out, in_, ident, reads, writes):
        return self.op('pe', lambda e: e.transpose(out=out, in_=in_, identity=ident), reads, writes)

    def act(self, out, in_, func, reads, writes, bias=None, scale=None):
        kw = {}
        if bias is not None:
            kw['bias'] = bias
        if scale is not None:
            kw['scale'] = scale
        return self.op('act', lambda e: e.activation(out=out, in_=in_, func=func, **kw), reads, writes)

    def tt(self, eng, out, in0, in1, op, reads, writes):
        return self.op(eng, lambda e: e.tensor_tensor(out=out, in0=in0, in1=in1, op=op), reads, writes)

    def ts(self, eng, out, in0, s1, op0, reads, writes, s2=None, op1=None):
        if op1 is None:
            return self.op(eng, lambda e: e.tensor_scalar(out=out, in0=in0, scalar1=s1, scalar2=None, op0=op0), reads, writes)
        return self.op(eng, lambda e: e.tensor_scalar(out=out, in0=in0, scalar1=s1, scalar2=s2, op0=op0, op1=op1), reads, writes)

    def stt(self, eng, out, in0, scalar, in1, op0, op1, reads, writes):
        return self.op(eng, lambda e: e.scalar_tensor_tensor(out=out, in0=in0, scalar=scalar, in1=in1, op0=op0, op1=op1),
                       reads, writes)

    def cp(self, eng, out, in_, reads, writes):
        if eng == 'act':
            return self.act(out, in_, AF.Copy, reads, writes)
        return self.op(eng, lambda e: e.tensor_copy(out=out, in_=in_), reads, writes)

    def memset(self, eng, ap, val, writes):
        return self.op(eng, lambda e: e.memset(ap, val), (), writes)

    def recip(self, out, in_, reads, writes):
        return self.op('dve', lambda e: e.reciprocal(out=out, in_=in_), reads, writes)

    def scan(self, out, d0, d1, init, reads, writes):
        return self.op('dve', lambda e: e.tensor_tensor_scan(out=out, data0=d0, data1=d1, initial=init,
                                                            op0=ALU.mult, op1=ALU.add), reads, writes)

    def fence(self, eng='sp'):
        o = self.op(eng, None)
        for p in self.slot_last.values():
            o.waits.append(p)
        return o

    def emit(self):
        nc = self.nc
        for e in ENGS:
            c = 0
            for o in self.ops[e]:
                if o.signal and not o.dma:
                    c += 1
                    o.sigidx = c
        esem = {e: self.es.enter_context(nc.semaphore("s_" + e)) for e in ENGS}
        dsem = {s: self.es.enter_context(nc.semaphore("d_%s_%d" % s)) for s in self.slot_count}

        def run(e, engine):
            waited = {}
            for o in self.ops[e]:
                ws = list(o.waits)
                if o.dma and o.prev_slot_op is not None:
                    ws.append(o.prev_slot_op)
                for p in ws:
                    if p.dma:
                        key, sem, val = p.slot, dsem[p.slot], p.val
                    else:
                        key, sem, val = p.eng, esem[p.eng], p.sigidx
                    if waited.get(key, 0) >= val:
                        continue
                    waited[key] = val
                    engine.wait_ge(sem, val)
                if o.fn is None:
                    continue
                ins = o.fn(engine)
                if o.dma:
                    ins.then_inc(dsem[o.slot], 16)
                elif o.signal:
                    ins.then_inc(esem[e], 1)

        with nc.Block() as block:
            if self.ops['sp']:
                block.sync(lambda eng: run('sp', eng))
            if self.ops['pe']:
                block.tensor(lambda eng: run('pe', eng))
            if self.ops['act']:
                block.scalar(lambda eng: run('act', eng))
            if self.ops['dve']:
                block.vector(lambda eng: run('dve', eng))
            if self.ops['pool']:
                block.gpsimd(lambda eng: run('pool', eng))
        self.es.close()


def tile_w(W):
    K, N = W.shape
    return np.ascontiguousarray(W.reshape(K // 128, 128, N // 128, 128).transpose(2, 1, 0, 3))


def col_vec(v):
    return np.ascontiguousarray(v.reshape(-1, 128).T)


def fm(ap, p=128):
    return ap.rearrange("(kc p) t -> p kc t", p=p)


class Ctx:
    pass


def emit_rmsnorm(P, C, h, gbuf, goff, out, nchunk=KC, n=512, eps=NORM_EPS, dim=D_MODEL):
    ps = C.pstat.next()
    for kc in range(nchunk):
        sq = C.sq.next()
        P.act(sq[:, :n], h[:, kc, :n], AF.Square, [h], [sq])
        P.mm(ps[:, :n], C.ones_bf[:, :], sq[:, :n], kc == 0, kc == nchunk - 1, [sq, C.ones_bf], [ps])
    rs = C.rstd.next()
    P.act(rs[:, :n], ps[:, :n], AF.Ln, [ps], [rs], bias=C.eps_t[:, 0:1], scale=1.0 / dim)
    P.act(rs[:, :n], rs[:, :n], AF.Exp, [rs], [rs], scale=-0.5)
    for kc in range(nchunk):
        P.stt('dve', out[:, kc, :n], h[:, kc, :n], gbuf[:, goff + kc:goff + kc + 1], rs[:, :n],
              ALU.mult, ALU.mult, [h, rs, gbuf], [out])


def load_consts(P, C, need_eps=(NORM_EPS,)):
    C.ones_bf = P.sb("ones_bf", [128, 128], BF16)
    P.memset('pool', C.ones_bf[:, :], 1.0, [C.ones_bf])
    C.eps_t = P.sb("eps_t", [128, 4])
    P.memset('pool', C.eps_t[:, 0:1], NORM_EPS, [C.eps_t])
    P.memset('pool', C.eps_t[:, 1:2], 1e-5, [C.eps_t])
    P.memset('pool', C.eps_t[:, 2:3], 1.0, [C.eps_t])
    P.memset('pool', C.eps_t[:, 3:4], 0.0, [C.eps_t])


def emit_N(P, G, T, bgm=lambda: None):
    P.begin_phase()
    C = Ctx()
    load_consts(P, C)
    C.pstat = Pool([P.ps("pstat%d" % i, [128, 512]) for i in range(2)])
    C.sq = P.sbpool("sq", [128, 512], BF16, 3)
    C.rstd = P.sbpool("rstd", [128, 512], F32, 2)
    g = P.sb("g", [128, KC])
    P.dma('sp', g[:, :], G.gains.t.ap()[:, 0:KC], [], [g])
    hp = P.sbpool("h", [128, KC, 512], F32, 2)
    op = P.sbpool("o", [128, KC, 512], BF16, 2)
    for blk in range(T // 512):
        tok = slice(blk * 512, (blk + 1) * 512)
        h = hp.next()
        P.dma('sp', h[:, :, :], fm(G.xT.t.ap())[:, :, tok], [], [h])
        bgm()
        o = op.next()
        emit_rmsnorm(P, C, h, g, 0, o)
        P.dma('sp', fm(G.hnT.t.ap())[:, :, tok], o[:, :, :], [o], [G.hn_dep[blk]])
    P.end_phase()


TW = (("w_out", 16, 16), ("w_up", 64, 16), ("w_dn", 16, 64), ("w_g", 16, 16), ("w_p", 16, 2))


def emit_T(P, G, i, T, last, bgm=lambda: None):
    P.begin_phase()
    C = Ctx()
    load_consts(P, C)
    hsrc = G.xT if i == 0 else G.hT
    C.pstat = Pool([P.ps("pstat%d" % k, [128, 512]) for k in range(2)])
    pmm = Pool([P.ps("pmm%d" % k, [128, 512]) for k in range(6)])
    C.sq = P.sbpool("sq", [128, 512], BF16, 2)
    C.rstd = P.sbpool("rstd", [128, 512], F32, 1)
    g = P.sb("g", [128, 3 * KC])
    P.dma('sp', g[:, :], G.gains.t.ap()[:, KC + 3 * KC * i:KC + 3 * KC * (i + 1)], [], [g])
    hp = P.sbpool("h", [128, KC, 512], F32, 2)
    hnp = P.sbpool("hn", [128, KC, 512], BF16, 1)
    ap_ = P.sbpool("a", [128, 64, 512], BF16, 1)
    pp = P.sbpool("p", [128, 2, 512], BF16, 2)
    w16 = P.sbpool("w16", [128, 16, 128], BF16, 4)
    w64 = P.sbpool("w64", [128, 64, 128], BF16, 2)
    w2 = P.sbpool("w2", [128, 2, 128], BF16, 2)
    rt = P.sbpool("rt", [128, 512], F32, 2)
    NBLK = T // 512
    a = ap_.next()
    hs, pbs = {}, {}

    def load_hp(b):
        tk = slice(b * 512, (b + 1) * 512)
        hs[b] = hp.next()
        P.dma('sp', hs[b][:, :, :], fm(hsrc.t.ap())[:, :, tk], [G.h_dep[b]], [hs[b]])
        pbs[b] = pp.next()
        P.dma('pool', pbs[b][:, :, :], fm(G.pT.t.ap()[i])[:, :, tk], [], [pbs[b]])

    def load_y(b):
        tk = slice(b * 512, (b + 1) * 512)
        P.dma('sp', a[:, 0:KC, :], fm(G.yT.t.ap())[:, :, tk], [G.y_dep[b]], [a])

    load_hp(0)
    load_y(0)

    def wload(name, j, w):
        P.dma('sp', w[:, :, :], G.wbf[name].t.ap()[i, j], [G.wdep[(name, i, j)]], [w])

    for blk in range(T // 512):
        tok = slice(blk * 512, (blk + 1) * 512)
        h = hs.pop(blk)
        pb = pbs.pop(blk)
        bgm()
        for j in range(16):
            w = w16.next()
            wload("w_out", j, w)
            ps = pmm.next()
            for kc in range(KC):
                P.mm(ps[:, :], w[:, kc, :], a[:, kc, :], kc == 0, kc == KC - 1, [w, a], [ps])
            P.tt('dve', h[:, j, :], h[:, j, :], ps[:, :], ALU.add, [h, ps], [h])
        if blk + 1 < NBLK:
            load_hp(blk + 1)
        hn = hnp.next()
        emit_rmsnorm(P, C, h, g, 0, hn)
        for j in range(64):
            w = w16.next()
            wload("w_up", j, w)
            ps = pmm.next()
            for kc in range(KC):
                P.mm(ps[:, :], w[:, kc, :], hn[:, kc, :], kc == 0, kc == KC - 1, [w, hn], [ps])
            r = rt.next()
            P.act(r[:, :], ps[:, :], AF.Relu, [ps], [r])
            P.tt('dve', a[:, j, :], r[:, :], r[:, :], ALU.mult, [r], [a])
        for j in range(16):
            w = w64.next()
            wload("w_dn", j, w)
            ps = pmm.next()
            for kc in range(64):
                P.mm(ps[:, :], w[:, kc, :], a[:, kc, :], kc == 0, kc == 63, [w, a], [ps])
            P.tt('dve', h[:, j, :], h[:, j, :], ps[:, :], ALU.add, [h, ps], [h])
        if blk + 1 < NBLK:
            load_y(blk + 1)
        emit_rmsnorm(P, C, h, g, KC, hn)
        for j in range(16):
            w = w16.next()
            wload("w_g", j, w)
            wp = w2.next()
            wload("w_p", j, wp)
            ps = pmm.next()
            for kc in range(KC):
                P.mm(ps[:, :], w[:, kc, :], hn[:, kc, :], kc == 0, kc == KC - 1, [w, hn], [ps])
            gt = rt.next()
            P.act(gt[:, :], ps[:, :], AF.Sigmoid, [ps], [gt])
            ps2 = pmm.next()
            for kc in range(2):
                P.mm(ps2[:, :], wp[:, kc, :], pb[:, kc, :], kc == 0, kc == 1, [wp, pb], [ps2])
            P.tt('dve', gt[:, :], gt[:, :], ps2[:, :], ALU.mult, [gt, ps2], [gt])
            P.tt('pool', h[:, j, :], h[:, j, :], gt[:, :], ALU.add, [h, gt], [h])
        o = h if last else hn
        emit_rmsnorm(P, C, h, g, 2 * KC, o)
        if last:
            P.dma('sp', fm(G.outT.t.ap())[:, :, tok], o[:, :, :], [o], [Dep()])
        else:
            P.dma('sp', fm(G.hT.t.ap())[:, :, tok], h[:, :, :], [h], [G.h_dep[blk]])
            P.dma('sp', fm(G.hnT.t.ap())[:, :, tok], o[:, :, :], [o], [G.hn_dep[blk]])
    P.end_phase()


CH = 64


def me_consts():
    i = np.arange(128)
    blk = i // CH
    same = blk[:, None] == blk[None, :]
    ident = np.eye(128)
    U = ((i[:, None] <= i[None, :]) & same)
    Lgt = (i[:, None] > i[None, :])
    ones_bd = same
    m_strict = ((i[:, None] > i[None, :]) & same)
    mT_strict = ((i[:, None] < i[None, :]) & same)
    mT_incl = ((i[:, None] <= i[None, :]) & same)
    reset = (np.arange(512) % CH != 0)
    tab = np.concatenate([ident, U, Lgt, ones_bd, m_strict, mT_strict, mT_incl,
                          np.broadcast_to(reset[None, :], (128, 512))], axis=1)
    return np.ascontiguousarray(tab.astype(np.float32))


def emit_ME(P, G, j, T, NH, bg):
    P.begin_phase()
    C = Ctx()
    NSEG = T // 512
    hnT, yT = G.hnT, G.yT
    w_in = ('e_w_in', G.mwbf['e_w_in'].t.ap()[j])
    w_ab_rep = ('e_w_ab_rep', G.mwbf['e_w_ab_rep'].t.ap()[j])
    w_ab_tok = G.e_w_ab_tok.t.ap()[j]
    cvec_d = G.e_cvec.t.ap()[j]
    lru_w = G.e_lru_w.t.ap()[j]
    ctab_d = G.e_ctab.t.ap()
    o_lcw, o_lcb, o_ba, o_bx, o_lam, o_nw, o_alog, o_dtb = 12 * NH, 16 * NH, 17 * NH, 18 * NH, 19 * NH, 20 * NH, 20 * NH + 1, 21 * NH + 1
    load_consts(P, C)
    ctab = P.sb("ctab_sb", [128, 7 * 128 + 512])
    P.dma('sp', ctab[:, :], ctab_d, [], [ctab])
    ident, U, Lgt, onesbd, m_strict, mT_strict, mT_incl = [ctab[:, k * 128:(k + 1) * 128] for k in range(7)]
    reset = ctab[:, 7 * 128:7 * 128 + 512]
    cv = P.sb("cv_sb", [128, 22 * NH + 1])
    P.dma('sp', cv[:, :], cvec_d, [], [cv])
    wab = P.sb("wab", [128, 16, 2 * NH], BF16)
    P.dma('pool', wab[:, :, :], w_ab_tok, [], [wab])
    lw = P.sb("lw", [128, 2 * NH, 128], BF16)
    for k in range(2 * NH):
        P.dma('pool', lw[:, k, :], lru_w[k], [], [lw])
    der = P.sb("der", [128, 3 * NH])
    P.act(der[:, 0:NH], cv[:, o_alog:o_alog + NH], AF.Exp, [cv], [der])
    P.ts('dve', der[:, 0:NH], der[:, 0:NH], -1.0, ALU.mult, [der], [der])
    P.act(der[:, NH:2 * NH], cv[:, o_lam:o_lam + NH], AF.Exp, [cv], [der], scale=-1.0)
    P.act(der[:, NH:2 * NH], der[:, NH:2 * NH], AF.Ln, [der], [der], bias=C.eps_t[:, 2:3])
    P.ts('dve', der[:, 2 * NH:3 * NH], der[:, NH:2 * NH], -16.0, ALU.mult, [der], [der])
    P.ts('dve', der[:, NH:2 * NH], der[:, NH:2 * NH], -8.0, ALU.mult, [der], [der])
    dtb_t = P.sb("dtb_t", [128, 4, NH])
    negA_t = P.sb("negA_t", [128, 4, NH])
    for p_ in range(4):
        P.cp('dve', dtb_t[:, p_, :], cv[:, o_dtb:o_dtb + NH], [cv], [dtb_t])
        P.cp('dve', negA_t[:, p_, :], der[:, 0:NH], [der], [negA_t])
    mask4 = P.sb("mask4", [128, 4 * 512])
    for k_, src in enumerate((ident, m_strict, mT_strict, mT_incl)):
        for p_ in range(4):
            P.cp('dve', mask4[:, k_ * 512 + p_ * 128:k_ * 512 + (p_ + 1) * 128], src, [ctab], [mask4])
    tails = P.sb("tails", [128, 4 * NH, 3])
    P.memset('dve', tails[:, :, :], 0.0, [tails])
    S = [P.sb("S%d" % h, [128, 128]) for h in range(NH)]
    Sb = [P.sb("Sb%d" % h, [128, 128], BF16) for h in range(NH)]
    for h in range(NH):
        P.memset('dve', S[h][:, :], 0.0, [S[h]])
        P.memset('dve', Sb[h][:, :], 0.0, [Sb[h]])
    hprev = P.sb("hprev", [128, NH])
    P.memset('dve', hprev[:, :], 0.0, [hprev])
    pmm = Pool([P.ps("pmm%d" % i, [128, 512]) for i in range(2)])
    C.pstat = Pool([P.ps("pstat%d" % i, [128, 512]) for i in range(1)])
    psm = Pool([P.ps("psm%d" % i, [128, 512]) for i in range(5)])
    C.sq = P.sbpool("sq", [128, 512], BF16, 2)
    hnp = P.sbpool("hn", [128, KC, 512], BF16, 2)
    w16 = P.sbpool("w16", [128, 16, 128], BF16, 6)
    bigA = P.sbpool("bigA", [128, 515], F32, 10)
    bigL = P.sbpool("bigL", [128, 515], F32, 8)
    bigB = P.sbpool("bigB", [128, 515], F32, 2)
    pools = {}

    def role(name, shape, dtype, n=2):
        if name not in pools:
            pools[name] = P.sbpool(name, shape, dtype, n)
        return pools[name].next()

    def inproj(wd, idx, hn):
        w = w16.next()
        wname, wap = wd
        P.dma('sp', w[:, :, :], wap[idx], [G.mwdep[(wname, j, idx)]], [w])
        ps = pmm.next()
        for kc in range(KC):
            P.mm(ps[:, :], w[:, kc, :], hn[:, kc, :], kc == 0, kc == KC - 1, [w, hn], [ps])
            if kc % 4 == 3 and kc != KC - 1:
                yield
        return ps

    def conv(ps, tidx, cwoff, big, bias_col=None):
        pre = big.next()
        P.cp('dve', pre[:, 0:3], tails[:, tidx, :], [tails], [pre])
        P.cp('act', pre[:, 3:515], ps[:, :], [ps], [pre])
        yield
        acc = big.next()
        if bias_col is None:
            P.ts('dve', acc[:, 0:512], pre[:, 0:512], cv[:, cwoff:cwoff + 1], ALU.mult, [pre, cv], [acc])
        else:
            P.ts('dve', acc[:, 0:512], pre[:, 0:512], cv[:, cwoff:cwoff + 1], ALU.mult, [pre, cv], [acc],
                 s2=cv[:, bias_col:bias_col + 1], op1=ALU.add)
        for j in range(1, 4):
            yield
            P.stt('dve', acc[:, 0:512], pre[:, j:j + 512], cv[:, cwoff + j:cwoff + j + 1], acc[:, 0:512], ALU.mult, ALU.add,
                  [pre, cv, acc], [acc])
        P.cp('dve', tails[:, tidx, :], pre[:, 512:515], [pre], [tails])
        yield
        return acc

    def bcast_sumsq(x):
        sq = C.sq.next()
        P.act(sq[:, :], x[:, 0:512], AF.Square, [x], [sq])
        ps = C.pstat.next()
        P.mm(ps[:, :], C.ones_bf[:, :], sq[:, :], True, True, [sq, C.ones_bf], [ps])
        return ps

    def seg_prologue(seg):
        sc = Ctx()
        sc.seg = seg
        sc.tok = slice(seg * 512, (seg + 1) * 512)
        hn = sc.hn = hnp.next()
        P.dma('sp', hn[:, :, :], fm(hnT.t.ap())[:, :, sc.tok], [G.hn_dep[seg]], [hn])
        bg()
        psr = psm.next()
        for p_ in range(4):
            for kc in range(KC):
                P.mm(psr[:, p_ * 2 * NH:(p_ + 1) * 2 * NH], hn[:, kc, p_ * 128:(p_ + 1) * 128], wab[:, kc, :], kc == 0, kc == KC - 1,
                     [hn, wab], [psr])
        raw = role("raw", [128, 4, 2 * NH], F32)
        P.cp('dve', raw[:, :, :], psr[:, 0:8 * NH].rearrange("p (a b) -> p a b", b=2 * NH), [psr], [raw])
        beta_tok = sc.beta_tok = role("beta_tok", [128, 4, NH], F32)
        P.act(beta_tok[:, :, :], raw[:, :, 0:NH], AF.Sigmoid, [raw], [beta_tok])
        xs = role("xs", [128, 4, NH], F32)
        P.tt('dve', xs[:, :, :], raw[:, :, NH:2 * NH], dtb_t[:, :, :], ALU.add, [raw, dtb_t], [xs])
        ax = role("ax", [128, 4, NH], F32)
        P.act(ax[:, :, :], xs[:, :, :], AF.Abs, [xs], [ax])
        P.act(ax[:, :, :], ax[:, :, :], AF.Exp, [ax], [ax], scale=-1.0)
        P.act(ax[:, :, :], ax[:, :, :], AF.Ln, [ax], [ax], bias=C.eps_t[:, 2:3])
        P.ts('dve', xs[:, :, :], xs[:, :, :], 0.0, ALU.max, [xs], [xs])
        P.tt('dve', xs[:, :, :], xs[:, :, :], ax[:, :, :], ALU.add, [xs, ax], [xs])
        g_tok = sc.g_tok = role("g_tok", [128, 4, NH], F32)
        P.tt('dve', g_tok[:, :, :], xs[:, :, :], negA_t[:, :, :], ALU.mult, [xs, negA_t], [g_tok])
        psc = psm.next()
        for p_ in range(4):
            P.mm(psc[:, p_ * NH:(p_ + 1) * NH], U, g_tok[:, p_, :], True, True, [ctab, g_tok], [psc])
            P.mm(psc[:, 4 * NH + p_ * NH:4 * NH + (p_ + 1) * NH], onesbd, g_tok[:, p_, :], True, True, [ctab, g_tok], [psc])
        gc_tok = role("gc_tok", [128, 8 * NH], F32)
        P.cp('dve', gc_tok[:, :], psc[:, 0:8 * NH], [psc], [gc_tok])
        egc_tok = role("egc_tok", [128, 4 * NH], F32)
        P.act(egc_tok[:, :], gc_tok[:, 0:4 * NH], AF.Exp, [gc_tok], [egc_tok])
        bg_tok = sc.bg_tok = role("bg_tok", [128, 4 * NH], F32)
        P.tt('dve', bg_tok[:, :], egc_tok[:, :], beta_tok[:, :, :].rearrange("p a b -> p (a b)"), ALU.mult,
             [egc_tok, beta_tok], [bg_tok])
        dec_tok = sc.dec_tok = role("dec_tok", [128, 4 * NH], F32)
        P.tt('dve', dec_tok[:, :], gc_tok[:, 4 * NH:8 * NH], gc_tok[:, 0:4 * NH], ALU.subtract, [gc_tok], [dec_tok])
        P.act(dec_tok[:, :], dec_tok[:, :], AF.Exp, [dec_tok], [dec_tok])
        return sc

    def stageA(sc, h, a):
        hn = sc.hn
        ps = yield from inproj(w_in, 4 * h + 0, hn)
        cq = yield from conv(ps, 3 * h + 0, 4 * (3 * h + 0), bigA)
        yield
        P.act(cq[:, 0:512], cq[:, 0:512], AF.Silu, [cq], [cq])
        yield
        ps = bcast_sumsq(cq)
        rn = bigA.next()
        P.act(rn[:, 0:512], ps[:, :], AF.Ln, [ps], [rn], bias=C.eps_t[:, 0:1])
        yield
        P.act(rn[:, 0:512], rn[:, 0:512], AF.Exp, [rn], [rn], scale=-0.5)
        yield
        qn = role("qn", [128, 512], F32)
        P.stt('dve', qn[:, :], cq[:, 0:512], 128.0 ** -0.5, rn[:, 0:512], ALU.mult, ALU.mult, [cq, rn], [qn])
        yield
        q_bf = a['q_bf'] = role("q_bf", [128, 512], BF16)
        P.cp('act', q_bf[:, :], qn[:, :], [qn], [q_bf])
        yield
        ps = yield from inproj(w_in, 4 * h + 1, hn)
        ck = yield from conv(ps, 3 * h + 1, 4 * (3 * h + 1), bigA)
        yield
        P.act(ck[:, 0:512], ck[:, 0:512], AF.Silu, [ck], [ck])
        yield
        ps = bcast_sumsq(ck)
        rn2 = bigA.next()
        P.act(rn2[:, 0:512], ps[:, :], AF.Ln, [ps], [rn2], bias=C.eps_t[:, 0:1])
        yield
        P.act(rn2[:, 0:512], rn2[:, 0:512], AF.Exp, [rn2], [rn2], scale=-0.5)
        yield
        kn = a['kn'] = role("kn", [128, 512], F32)
        P.tt('dve', kn[:, :], ck[:, 0:512], rn2[:, 0:512], ALU.mult, [ck, rn2], [kn])
        yield
        k_bf = a['k_bf'] = role("k_bf", [128, 512], BF16)
        P.cp('act', k_bf[:, :], kn[:, :], [kn], [k_bf])
        yield
        ps = yield from inproj(w_in, 4 * h + 2, hn)
        cvv = yield from conv(ps, 3 * h + 2, 4 * (3 * h + 2), bigA)
        c_v = a['c_v'] = role("c_v", [128, 512], F32)
        P.act(c_v[:, :], cvv[:, 0:512], AF.Silu, [cvv], [c_v])
        yield
        sz = a['sz'] = role("sz", [128, 512], F32)
        ps = yield from inproj(w_in, 4 * h + 3, hn)
        P.act(sz[:, :], ps[:, :], AF.Silu, [ps], [sz])
        yield
        beta_b = bigA.next()
        ps = yield from inproj(w_ab_rep, 2 * h, hn)
        P.act(beta_b[:, 0:512], ps[:, :], AF.Sigmoid, [ps], [beta_b])
        kb_bf = a['kb_bf'] = role("kb_bf", [128, 512], BF16)
        P.tt('dve', kb_bf[:, :], kn[:, :], beta_b[:, 0:512], ALU.mult, [kn, beta_b], [kb_bf])
        yield
        ps = yield from inproj(w_ab_rep, 2 * h + 1, hn)
        t1 = bigA.next()
        P.act(t1[:, 0:512], ps[:, :], AF.Abs, [ps, cv], [t1], bias=cv[:, o_dtb + h:o_dtb + h + 1])
        t4 = bigA.next()
        P.ts('dve', t4[:, 0:512], ps[:, :], cv[:, o_dtb + h:o_dtb + h + 1], ALU.add, [ps, cv], [t4], s2=0.0, op1=ALU.max)
        yield
        P.act(t1[:, 0:512], t1[:, 0:512], AF.Exp, [t1], [t1], scale=-1.0)
        P.act(t1[:, 0:512], t1[:, 0:512], AF.Ln, [t1], [t1], bias=C.eps_t[:, 2:3])
        yield
        P.tt('dve', t1[:, 0:512], t1[:, 0:512], t4[:, 0:512], ALU.add, [t1, t4], [t1])
        P.ts('dve', t1[:, 0:512], t1[:, 0:512], der[:, h:h + 1], ALU.mult, [t1, der], [t1])
        yield
        gc_b = bigA.next()
        P.scan(gc_b[:, 0:512], reset, t1[:, 0:512], 0.0, [ctab, t1], [gc_b])
        yield
        egc_b = a['egc_b'] = role("egc_b", [128, 512], F32)
        P.act(egc_b[:, :], gc_b[:, 0:512], AF.Exp, [gc_b], [egc_b])
        qd_bf = a['qd_bf'] = role("qd_bf", [128, 512], BF16)
        P.tt('dve', qd_bf[:, :], qn[:, :], egc_b[:, :], ALU.mult, [qn, egc_b], [qd_bf])
        yield

    def stageL(sc, g):
        hn = sc.hn
        ps = yield from inproj(w_in, 4 * NH + 2 * g, hn)
        xc = yield from conv(ps, 3 * NH + g, o_lcw + 4 * g, bigL, bias_col=o_lcb + g)
        yield
        ps = yield from inproj(w_in, 4 * NH + 2 * g + 1, hn)
        gy = role("gy", [128, 512], F32)
        P.act(gy[:, :], ps[:, :], AF.Gelu, [ps], [gy])
        xc_bf = role("xc_bf", [128, 512], BF16)
        P.cp('act', xc_bf[:, :], xc[:, 0:512], [xc], [xc_bf])
        yield
        ps = pmm.next()
        P.mm(ps[:, :], lw[:, g, :], xc_bf[:, :], True, True, [lw, xc_bf], [ps])
        rg = bigL.next()
        P.act(rg[:, 0:512], ps[:, :], AF.Sigmoid, [ps, cv], [rg], bias=cv[:, o_ba + g:o_ba + g + 1])
        ps = pmm.next()
        P.mm(ps[:, :], lw[:, NH + g, :], xc_bf[:, :], True, True, [lw, xc_bf], [ps])
        ig = bigL.next()
        P.act(ig[:, 0:512], ps[:, :], AF.Sigmoid, [ps, cv], [ig], bias=cv[:, o_bx + g:o_bx + g + 1])
        yield
        aa = bigL.next()
        P.act(aa[:, 0:512], rg[:, 0:512], AF.Exp, [rg, der], [aa], scale=der[:, NH + g:NH + g + 1])
        a2 = bigL.next()
        P.act(a2[:, 0:512], rg[:, 0:512], AF.Exp, [rg, der], [a2], scale=der[:, 2 * NH + g:2 * NH + g + 1])
        P.act(a2[:, 0:512], a2[:, 0:512], AF.Sqrt, [a2], [a2], bias=C.eps_t[:, 2:3], scale=-1.0)
        yield
        P.tt('dve', ig[:, 0:512], ig[:, 0:512], xc[:, 0:512], ALU.mult, [ig, xc], [ig])
        P.tt('dve', ig[:, 0:512], ig[:, 0:512], a2[:, 0:512], ALU.mult, [ig, a2], [ig])
        yield
        hs = bigL.next()
        P.scan(hs[:, 0:512], aa[:, 0:512], ig[:, 0:512], hprev[:, g:g + 1], [aa, ig, hprev], [hs])
        P.cp('dve', hprev[:, g:g + 1], hs[:, 511:512], [hs], [hprev])
        yield
        y_bf = role("yl_bf", [128, 512], BF16)
        P.tt('dve', y_bf[:, :], hs[:, 0:512], gy[:, :], ALU.mult, [hs, gy], [y_bf])
        P.dma('pool', yT.t.ap()[128 * NH + g * 128:128 * NH + (g + 1) * 128, sc.tok], y_bf[:, :], [y_bf], [G.y_dep[sc.seg]])
        yield

    def stageB(sc, h, a):
        beta_tok, g_tok, bg_tok, dec_tok = sc.beta_tok, sc.g_tok, sc.bg_tok, sc.dec_tok
        c_v, kn, k_bf, kb_bf, q_bf, qd_bf, egc_b, sz = (a[k] for k in ('c_v', 'kn', 'k_bf', 'kb_bf', 'q_bf', 'qd_bf', 'egc_b', 'sz'))
        o_sb = role("o_sb", [128, 512], F32)
        cs = [slice(p_ * 128, (p_ + 1) * 128) for p_ in range(4)]
        bV4 = role("bV4", [128, 512], F32, 1)
        bkd4 = role("bkd4", [128, 512], F32, 1)
        kdec4 = role("kdec4", [128, 512], BF16, 1)
        gU4 = role("gU4", [128, 512], F32, 1)
        pst = psm.next()
        for p_ in range(4):
            P.tr(pst[:, cs[p_]], c_v[:, cs[p_]], ident, [c_v, ctab], [pst])
        for p_ in range(4):
            P.ts('dve', bV4[:, cs[p_]], pst[:, cs[p_]], beta_tok[:, p_, h:h + 1], ALU.mult, [pst, beta_tok], [bV4])
        yield
        pst = psm.next()
        for p_ in range(4):
            P.tr(pst[:, cs[p_]], kn[:, cs[p_]], ident, [kn, ctab], [pst])
        for p_ in range(4):
            col = p_ * NH + h
            P.ts('dve', bkd4[:, cs[p_]], pst[:, cs[p_]], bg_tok[:, col:col + 1], ALU.mult, [pst, bg_tok], [bkd4])
            P.act(kdec4[:, cs[p_]], pst[:, cs[p_]], AF.Copy, [pst, dec_tok], [kdec4], scale=dec_tok[:, col:col + 1])
        yield
        for p_ in range(4):
            P.ts('dve', gU4[:, cs[p_]], U, g_tok[:, p_, h:h + 1], ALU.mult, [ctab, g_tok], [gU4])
        psd = psm.next()
        for p_ in range(4):
            P.mm(psd[:, cs[p_]], gU4[:, cs[p_]], Lgt, True, True, [gU4, ctab], [psd])
        Dm4 = role("Dm4", [128, 512], F32, 1)
        P.act(Dm4[:, :], psd[:, :], AF.Exp, [psd], [Dm4])
        yield
        psd = psm.next()
        for p_ in range(4):
            P.mm(psd[:, cs[p_]], Lgt, gU4[:, cs[p_]], True, True, [gU4, ctab], [psd])
        DmT4 = role("DmT4", [128, 512], F32, 1)
        P.act(DmT4[:, :], psd[:, :], AF.Exp, [psd], [DmT4])
        yield
        psM = psm.next()
        for p_ in range(4):
            P.mm(psM[:, cs[p_]], kb_bf[:, cs[p_]], k_bf[:, cs[p_]], True, True, [kb_bf, k_bf], [psM])
        tq = role("tq4", [128, 512], F32, 2)
        P.tt('dve', tq[:, :], psM[:, :], Dm4[:, :], ALU.mult, [psM, Dm4], [tq])
        QT = role("QT4", [128, 512], F32, 2)
        P.stt('dve', QT[:, :], tq[:, :], -1.0, mask4[:, 512:1024], ALU.mult, ALU.mult, [tq, mask4], [QT])
        yield
        psM = psm.next()
        for p_ in range(4):
            P.mm(psM[:, cs[p_]], k_bf[:, cs[p_]], kb_bf[:, cs[p_]], True, True, [kb_bf, k_bf], [psM])
        tq = role("tq4", [128, 512], F32, 2)
        P.tt('dve', tq[:, :], psM[:, :], DmT4[:, :], ALU.mult, [psM, DmT4], [tq])
        Q = role("Q4", [128, 512], F32, 2)
        P.stt('dve', Q[:, :], tq[:, :], -1.0, mask4[:, 1024:1536], ALU.mult, ALU.mult, [tq, mask4], [Q])
        R = role("R4", [128, 512], F32, 2)
        P.tt('dve', R[:, :], Q[:, :], mask4[:, 0:512], ALU.add, [Q, mask4], [R])
        yield
        psA = psm.next()
        for p_ in range(4):
            P.mm(psA[:, cs[p_]], k_bf[:, cs[p_]], q_bf[:, cs[p_]], True, True, [k_bf, q_bf], [psA])
        tq = role("tq4", [128, 512], F32, 2)
        P.tt('dve', tq[:, :], psA[:, :], DmT4[:, :], ALU.mult, [psA, DmT4], [tq])
        AT4 = role("AT4", [128, 512], BF16, 1)
        P.tt('dve', AT4[:, :], tq[:, :], mask4[:, 1536:2048], ALU.mult, [tq, mask4], [AT4])
        yield
        for lvl in range(1, 6):
            psq = psm.next()
            for p_ in range(4):
                P.mm(psq[:, cs[p_]], Q[:, cs[p_]], QT[:, cs[p_]], True, True, [Q, QT], [psq])
            QTn = role("QT4", [128, 512], F32, 2)
            P.cp('act', QTn[:, :], psq[:, :], [psq], [QTn])
            yield
            if lvl < 5:
                psq = psm.next()
                for p_ in range(4):
                    P.mm(psq[:, cs[p_]], QT[:, cs[p_]], Q[:, cs[p_]], True, True, [Q, QT], [psq])
                Qn = role("Q4", [128, 512], F32, 2)
                P.cp('dve', Qn[:, :], psq[:, :], [psq], [Qn])
                Q = Qn
                yield
            QT = QTn
            psq = psm.next()
            for p_ in range(4):
                P.mm(psq[:, cs[p_]], QT[:, cs[p_]], R[:, cs[p_]], True, True, [QT, R], [psq])
            Rn = role("R4", [128, 512], F32, 2)
            P.tt('dve', Rn[:, :], psq[:, :], R[:, :], ALU.add, [psq, R], [Rn])
            R = Rn
            yield
        psu = psm.next()
        for p_ in range(4):
            P.mm(psu[:, cs[p_]], R[:, cs[p_]], bV4[:, cs[p_]], True, True, [R, bV4], [psu])
        u4 = role("u4", [128, 512], F32, 1)
        P.cp('act', u4[:, :], psu[:, :], [psu], [u4])
        yield
        psw = psm.next()
        for p_ in range(4):
            P.mm(psw[:, cs[p_]], bkd4[:, cs[p_]], R[:, cs[p_]], True, True, [R, bkd4], [psw])
        wT4 = role("wT4", [128, 512], BF16, 1)
        P.cp('dve', wT4[:, :], psw[:, :], [psw], [wT4])
        yield
        prs = []
        for p_ in range(4):
            pr = Ctx()
            pr.wT, pr.u_sb, pr.kdec, pr.AT = wT4, u4, kdec4, AT4
            pr.c0 = p_ * 128
            prs.append(pr)
        for p_, pr in enumerate(prs):
            vn = role("vn", [128, 128], BF16)
            for c in range(2):
                r0 = slice(c * CH, (c + 1) * CH)
                tc_ = slice(p_ * 128 + c * CH, p_ * 128 + (c + 1) * CH)
                last = p_ * 128 + (c + 1) * CH - 1
                psws = psm.next()
                P.mm(psws[:, 0:128], pr.wT[:, pr.c0:pr.c0 + 128], Sb[h][:, :], True, True, [pr.wT, Sb[h]], [psws])
                P.tt('dve', vn[r0, :], pr.u_sb[r0, pr.c0:pr.c0 + 128], psws[r0, 0:128], ALU.subtract, [pr.u_sb, psws], [vn])
                yield
                pskv = psm.next()
                P.mm(pskv[:, 0:128], pr.kdec[r0, pr.c0:pr.c0 + 128], vn[r0, :], True, True, [pr.kdec, vn], [pskv])
                pso = psm.next()
                P.mm(pso[:, 0:CH], Sb[h][:, :], qd_bf[:, tc_], True, False, [Sb[h], qd_bf], [pso])
                P.mm(pso[:, 0:CH], vn[r0, :], pr.AT[r0, pr.c0 + c * CH:pr.c0 + (c + 1) * CH], False, True, [vn, pr.AT], [pso])
                P.stt('dve', Sb[h][:, :], S[h][:, :], egc_b[:, last:last + 1], pskv[:, 0:128], ALU.mult, ALU.add,
                      [S[h], egc_b, pskv], [Sb[h]])
                P.stt('dve', S[h][:, :], S[h][:, :], egc_b[:, last:last + 1], pskv[:, 0:128], ALU.mult, ALU.add,
                      [S[h], egc_b, pskv], [S[h]])
                P.cp('act', o_sb[:, tc_], pso[:, 0:CH], [pso], [o_sb])
                yield
        ps = bcast_sumsq(o_sb)
        rs = bigB.next()
        P.act(rs[:, 0:512], ps[:, :], AF.Ln, [ps], [rs], bias=C.eps_t[:, 0:1], scale=1.0 / 128)
        P.act(rs[:, 0:512], rs[:, 0:512], AF.Exp, [rs], [rs], scale=-0.5)
        yield
        P.stt('dve', rs[:, 0:512], o_sb[:, :], cv[:, o_nw:o_nw + 1], rs[:, 0:512], ALU.mult, ALU.mult, [o_sb, cv, rs], [rs])
        y_bf = role("y_bf", [128, 512], BF16)
        P.tt('dve', y_bf[:, :], rs[:, 0:512], sz[:, :], ALU.mult, [rs, sz], [y_bf])
        P.dma('pool', yT.t.ap()[h * 128:(h + 1) * 128, sc.tok], y_bf[:, :], [y_bf], [G.y_dep[sc.seg]])
        yield

    done = []

    def bulk_gen():
        for seg in range(NSEG):
            sc = seg_prologue(seg)
            yield
            for h in range(NH):
                a = {}
                yield from stageA(sc, h, a)
                done.append((a, sc))
                yield
                yield from stageL(sc, h)

    bulk = bulk_gen()
    state = {'alive': True}

    def pump():
        if state['alive']:
            try:
                next(bulk)
            except StopIteration:
                state['alive'] = False

    idx = 0
    RATIO = 1
    for seg in range(NSEG):
        for h in range(NH):
            while len(done) <= idx:
                pump()
            a, sc = done[idx]
            idx += 1
            tick = 0
            for _ in stageB(sc, h, a):
                tick += 1
                for _k in range(2):
                    if len(done) <= idx:
                        pump()
    while state['alive']:
        pump()
    P.end_phase()


def me_host_inputs(NH, ev_w_in, dn_conv_w, dn_a_log, dn_dt_bias, dn_norm_w, lru_conv_w, lru_conv_b, lru_wa, lru_ba,
                   lru_wx, lru_bx, lru_lambda):
    cols = []
    for h in range(NH):
        for base in (0, 1024, 2048, 3072):
            cols.append(np.arange(base + 128 * h, base + 128 * h + 128))
    for g in range(NH):
        cols.append(np.arange(4112 + 128 * g, 4112 + 128 * g + 128))
        cols.append(np.arange(5136 + 128 * g, 5136 + 128 * g + 128))
    w_in = tile_w(ev_w_in[:, np.concatenate(cols)])
    rep = []
    for h in range(NH):
        rep.append(np.full(128, 4096 + h))
        rep.append(np.full(128, 4104 + h))
    w_ab_rep = tile_w(ev_w_in[:, np.concatenate(rep)])
    abcols = [4096 + h for h in range(NH)] + [4104 + h for h in range(NH)]
    w_ab_tok = np.ascontiguousarray(ev_w_in[:, abcols].reshape(16, 128, 2 * NH).transpose(1, 0, 2))
    cv = np.zeros((128, 22 * NH + 1), np.float32)
    for h in range(NH):
        for xi, base in enumerate((0, 1024, 2048)):
            t = 3 * h + xi
            cv[:, 4 * t:4 * t + 4] = dn_conv_w[:, base + 128 * h:base + 128 * h + 128].T
    for g in range(NH):
        sl = slice(128 * g, 128 * g + 128)
        cv[:, 12 * NH + 4 * g:12 * NH + 4 * g + 4] = lru_conv_w[:, sl].T
        cv[:, 16 * NH + g] = lru_conv_b[sl]
        cv[:, 17 * NH + g] = lru_ba[sl]
        cv[:, 18 * NH + g] = lru_bx[sl]
        cv[:, 19 * NH + g] = lru_lambda[sl]
    cv[:, 20 * NH] = dn_norm_w
    for h in range(NH):
        cv[:, 20 * NH + 1 + h] = dn_a_log[h]
        cv[:, 21 * NH + 1 + h] = dn_dt_bias[h]
    lw = np.ascontiguousarray(np.concatenate([lru_wa[:NH], lru_wx[:NH]], axis=0))
    return {"w_in": w_in, "w_ab_rep": w_ab_rep, "w_ab_tok": w_ab_tok, "cvec": cv, "lru_w": lw}


SWA_BR = ((128, 1), (512, 4), (2048, 16))
NEG = -30000.0
RC = 128


def ss(start, n, step):
    return slice(start, start + (n - 1) * step + 1, step)


def mo_consts():
    iq = np.arange(128)[None, :]
    ik = np.arange(128)[:, None]
    tabs = [np.eye(128)]
    for (_, d) in SWA_BR:
        tabs += [np.where(ik >= iq, (128 + iq - ik) * float(d), 0.0), np.where(ik <= iq, (iq - ik) * float(d), 0.0)]
    tabs += [np.where(ik >= iq, 0.0, NEG), np.where(ik <= iq, 0.0, NEG)]
    for r in range(NHR_O):
        lg = np.log1p(-2.0 ** (-5.0 - r))
        rel = iq - ik
        dm = np.where(rel >= 0, np.exp(np.maximum(rel, 0) * lg), 0.0) * (128.0 ** -0.5)
        qdec = np.broadcast_to(np.exp((np.arange(128) + 1.0) * lg)[None, :], (128, 128))
        tabs += [dm, qdec]
    tab = np.concatenate(tabs, axis=1).astype(np.float32)
    cols = np.zeros((128, NHS_O + 2 * NHR_O), np.float32)
    for h in range(NHS_O):
        cols[:, h] = -(2.0 ** (-(h + 1)))
    for r in range(NHR_O):
        lg = np.log1p(-2.0 ** (-5.0 - r))
        cols[:, NHS_O + r] = np.exp((127.0 - np.arange(128)) * lg) * (128.0 ** -0.5)
        cols[:, NHS_O + NHR_O + r] = np.exp(128.0 * lg)
    return np.ascontiguousarray(tab), cols


def mo_host_inputs(od_w_in):
    cols = []
    for h in range(NHS_O):
        for base in (0, 1024, 2048):
            cols.append(np.arange(base + 128 * h, base + 128 * h + 128))
    for r in range(NHR_O):
        cols.append(np.arange(3072 + 128 * r, 3072 + 128 * r + 128))
        cols.append(np.arange(3584 + 128 * r, 3584 + 128 * r + 128))
        cols.append(np.arange(4096 + 256 * r, 4096 + 256 * r + 256))
        cols.append(np.arange(5120 + 256 * r, 5120 + 256 * r + 256))
    return {"w_in": tile_w(od_w_in[:, np.concatenate(cols)])}


def emit_MO(P, G, j, T, NHS, NHR, bg):
    P.begin_phase()
    C = Ctx()
    NBLK = T // 512
    NSB = T // 2048
    hnT, yT = G.hnT, G.yT
    w_in = G.o_w_in.t.ap()[j]
    NTAB = 9 + 2 * NHR
    ctab_d = G.o_ctab.t.ap()
    ccol_d = G.o_ccol.t.ap()
    load_consts(P, C)
    ctab = P.sb("ctab_sb", [128, NTAB * 128])
    P.dma('sp', ctab[:, :], ctab_d, [], [ctab])
    ccol = P.sb("ccol_sb", [128, NHS + 2 * NHR])
    P.dma('sp', ccol[:, :], ccol_d, [], [ccol])
    htab = P.sb("htab", [128, 6 * 128])
    ident_bf = P.sb("ident_bf", [128, 128], BF16)
    P.cp('dve', ident_bf[:, :], ctab[:, 0:128], [ctab], [ident_bf])
    ones_f = P.sb("ones_f", [128, 128])
    P.memset('dve', ones_f[:, :], 1.0, [ones_f])

    def btab(br, which):
        k = br * 2 + which
        return htab[:, k * 128:(k + 1) * 128]

    pmm = Pool([P.ps("pmm%d" % i, [128, 512]) for i in range(1)])
    pss = Pool([P.ps("pss%d" % i, [128, 512]) for i in range(2)])
    psn = Pool([P.ps("psn%d" % i, [128, 512]) for i in range(2)])
    psd = Pool([P.ps("psd%d" % i, [128, 512]) for i in range(2)])
    ptb = Pool([P.ps("ptb%d" % i, [128, 1024], BF16) for i in range(1)])
    hnp = P.sbpool("hn", [128, KC, 512], BF16, 2)
    w16 = P.sbpool("w16", [128, 16, 128], BF16, 6)
    pools = {}

    def role(name, shape, dtype, n=2):
        if name not in pools:
            pools[name] = P.sbpool(name, shape, dtype, n)
        return pools[name].next()

    qT = P.sb("qT", [128, T], BF16)
    kT = P.sb("kT", [128, T], BF16)
    vT = P.sb("vT", [128, T], BF16)
    for h in range(NHS):
        for br in range(3):
            for which in range(2):
                k = br * 2 + which
                P.stt('dve', htab[:, k * 128:(k + 1) * 128], ctab[:, (1 + k) * 128:(2 + k) * 128], ccol[:, h:h + 1],
                      ctab[:, (7 + which) * 128:(8 + which) * 128], ALU.mult, ALU.add, [ctab, ccol], [htab])
        ws = []
        for x in range(3):
            w = w16.next()
            P.dma('pool', w[:, :, :], w_in[3 * h + x], [], [w])
            ws.append(w)
        for blk in range(NBLK):
            tok = slice(blk * 512, (blk + 1) * 512)
            hn = hnp.next()
            P.dma('sp', hn[:, :, :], fm(hnT.t.ap())[:, :, tok], [G.hn_dep[blk]], [hn])
            bg()
            for x, dst in enumerate((qT, kT, vT)):
                ps = pmm.next()
                for kc in range(KC):
                    P.mm(ps[:, :], ws[x][:, kc, :], hn[:, kc, :], kc == 0, kc == KC - 1, [ws[x], hn], [ps])
                P.cp('act', dst[:, tok], ps[:, :], [ps], [dst])
        for sb_ in range(NSB):
            base = sb_ * 2048
            accN = role("accN", [128, 2048], F32, 1)
            accD = role("accD", [128, 2048], F32, 1)
            qblocks = []
            for br, (_, d) in enumerate(SWA_BR):
                nloc = 2048 // (128 * d)
                for r in range(d):
                    for nl in range(nloc):
                        qblocks.append((br, d, r, sb_ * nloc + nl, nl == 0))
            vcache = {}

            def vblock(d, r, nbk):
                key = (d, r, nbk)
                if key in vcache:
                    return vcache[key]
                ks = ss(nbk * 128 * d + r, 128, d)
                pt = ptb.next()
                P.tr(pt[:, 0:128], vT[:, ks], ident_bf[:, :], [vT, ident_bf], [pt])
                vb = role("vb", [128, 128], BF16, 8)
                P.cp('act', vb[:, :], pt[:, 0:128], [pt], [vb])
                vcache.clear()
                vcache[key] = vb
                return vb

            def stage1(qb):
                br, d, r, nb, first = qb
                qs = ss(nb * 128 * d + r, 128, d)
                c = Ctx()
                c.qb = qb
                c.vprev = vblock(d, r, nb - 1) if nb > 0 else None
                c.vcur = vblock(d, r, nb)
                lo = 0 if nb > 0 else 128
                ps = pss.next()
                if nb > 0:
                    P.mm(ps[:, 0:128], kT[:, ss((nb - 1) * 128 * d + r, 128, d)], qT[:, qs], True, True, [kT, qT], [ps])
                P.mm(ps[:, 128:256], kT[:, qs], qT[:, qs], True, True, [kT, qT], [ps])
                tb = role("tb", [128, 256], F32, 3)
                P.stt('dve', tb[:, lo:256], ps[:, lo:256], 128.0 ** -0.5, htab[:, br * 256 + lo:br * 256 + 256], ALU.mult, ALU.add,
                      [ps, htab], [tb])
                c.pT = role("pT", [128, 256], BF16, 3)
                P.act(c.pT[:, lo:256], tb[:, lo:256], AF.Exp, [tb], [c.pT])
                return c

            def stage2(c):
                br, d, r, nb, first = c.qb
                pn = psn.next()
                pd = psd.next()
                if nb > 0:
                    P.mm(pn[:, 0:128], c.vprev[:, :], c.pT[:, 0:128], True, False, [c.vprev, c.pT], [pn])
                    P.mm(pd[:, 0:128], C.ones_bf[:, :], c.pT[:, 0:128], True, False, [C.ones_bf, c.pT], [pd])
                P.mm(pn[:, 0:128], c.vcur[:, :], c.pT[:, 128:256], nb == 0, True, [c.vcur, c.pT], [pn])
                P.mm(pd[:, 0:128], C.ones_bf[:, :], c.pT[:, 128:256], nb == 0, True, [C.ones_bf, c.pT], [pd])
                ql = ss(nb * 128 * d + r - base, 128, d)
                if br == 0:
                    P.cp('act', accN[:, ql], pn[:, 0:128], [pn], [accN])
                    P.cp('dve', accD[:, ql], pd[:, 0:128], [pd], [accD])
                else:
                    P.tt('dve', accN[:, ql], accN[:, ql], pn[:, 0:128], ALU.add, [accN, pn], [accN])
                    P.tt('dve', accD[:, ql], accD[:, ql], pd[:, 0:128], ALU.add, [accD, pd], [accD])

            pend = None
            for qb in qblocks:
                cur = stage1(qb)
                if pend is not None:
                    stage2(pend)
                pend = cur
            stage2(pend)
            P.recip(accD[:, :], accD[:, :], [accD], [accD])
            ysw = role("ysw", [128, 2048], BF16, 1)
            P.tt('dve', ysw[:, :], accN[:, :], accD[:, :], ALU.mult, [accN, accD], [ysw])
            P.dma('sp', yT.t.ap()[h * 128:(h + 1) * 128, base:base + 2048], ysw[:, :], [ysw], [G.y_dep[4 * sb_ + q_] for q_ in range(4)])

    St = [P.sb("St%d" % r, [128, 256]) for r in range(NHR)]
    Stb = [P.sb("Stb%d" % r, [128, 256], BF16) for r in range(NHR)]
    for r in range(NHR):
        P.memset('dve', St[r][:, :], 0.0, [St[r]])
        P.memset('dve', Stb[r][:, :], 0.0, [Stb[r]])
    for r in range(NHR):
        dmT = ctab[:, (9 + 2 * r) * 128:(10 + 2 * r) * 128]
        qdec = ctab[:, (10 + 2 * r) * 128:(11 + 2 * r) * 128]
        ws = []
        for x in range(6):
            w = w16.next()
            P.dma('pool', w[:, :, :], w_in[3 * NHS + 6 * r + x], [], [w])
            ws.append(w)
        def rstageA(blk, outs):
            tok = slice(blk * 512, (blk + 1) * 512)
            hn = hnp.next()
            P.dma('sp', hn[:, :, :], fm(hnT.t.ap())[:, :, tok], [G.hn_dep[blk]], [hn])
            bg()
            for x in range(6):
                ps = pmm.next()
                for kc in range(KC):
                    P.mm(ps[:, :], ws[x][:, kc, :], hn[:, kc, :], kc == 0, kc == KC - 1, [ws[x], hn], [ps])
                    if kc % 4 == 3 and kc != KC - 1:
                        yield
                if x < 4:
                    o = role("rp%d" % x, [128, 512], BF16)
                    P.cp('act', o[:, :], ps[:, :], [ps], [o])
                else:
                    o = role("rp%d" % x, [128, 512], F32)
                    P.act(o[:, :], ps[:, :], AF.Silu, [ps], [o])
                outs.append(o)
                yield
        def rstageB(blk, outs):
            tok = slice(blk * 512, (blk + 1) * 512)
            rq, rk, rv0, rv1, sg0, sg1 = outs
            o_sb = [role("ro0", [128, 512], F32), role("ro1", [128, 512], F32)]
            for c in range(4):
                tp = slice(c * RC, (c + 1) * RC)
                ps = pss.next()
                P.mm(ps[:, 0:128], rk[:, tp], rq[:, tp], True, True, [rk, rq], [ps])
                pT = role("rpT", [128, 128], BF16)
                P.tt('dve', pT[:, :], ps[:, 0:128], dmT, ALU.mult, [ps, ctab], [pT])
                pt = ptb.next()
                P.tr(pt[:, 0:128], rv0[:, tp], ident_bf[:, :], [rv0, ident_bf], [pt])
                P.tr(pt[:, 128:256], rv1[:, tp], ident_bf[:, :], [rv1, ident_bf], [pt])
                P.tr(pt[:, 256:384], rk[:, tp], ident_bf[:, :], [rk, ident_bf], [pt])
                Vt = role("rVt", [128, 256], BF16)
                P.cp('act', Vt[:, :], pt[:, 0:256], [pt], [Vt])
                kd = role("rkd", [128, 128], BF16)
                P.ts('dve', kd[:, :], pt[:, 256:384], ccol[:, NHS + r:NHS + r + 1], ALU.mult, [pt, ccol], [kd])
                qd = role("rqd", [128, 128], BF16)
                P.tt('dve', qd[:, :], rq[:, tp], qdec, ALU.mult, [rq, ctab], [qd])
                yield
                for dvt in range(2):
                    po = psn.next()
                    P.mm(po[:, 0:128], Vt[:, dvt * 128:(dvt + 1) * 128], pT[:, :], True, False, [Vt, pT], [po])
                    P.mm(po[:, 0:128], Stb[r][:, dvt * 128:(dvt + 1) * 128], qd[:, :], False, True, [Stb[r], qd], [po])
                    P.cp('act', o_sb[dvt][:, tp], po[:, 0:128], [po], [o_sb[dvt]])
                    yield
                pk = psd.next()
                P.mm(pk[:, 0:256], kd[:, :], Vt[:, :], True, True, [kd, Vt], [pk])
                P.stt('dve', Stb[r][:, :], St[r][:, :], ccol[:, NHS + NHR + r:NHS + NHR + r + 1], pk[:, 0:256], ALU.mult, ALU.add,
                      [St[r], ccol, pk], [Stb[r]])
                P.stt('dve', St[r][:, :], St[r][:, :], ccol[:, NHS + NHR + r:NHS + NHR + r + 1], pk[:, 0:256], ALU.mult, ALU.add,
                      [St[r], ccol, pk], [St[r]])
                yield
            p1 = pss.next()
            P.mm(p1[:, :], ones_f[:, :], o_sb[0][:, :], True, False, [ones_f, o_sb[0]], [p1])
            P.mm(p1[:, :], ones_f[:, :], o_sb[1][:, :], False, True, [ones_f, o_sb[1]], [p1])
            mean = role("rmean", [128, 512], F32)
            P.act(mean[:, :], p1[:, :], AF.Copy, [p1], [mean], scale=1.0 / 256)
            yield
            p2 = pss.next()
            for dvt in range(2):
                sq = role("rsq", [128, 512], F32)
                P.act(sq[:, :], o_sb[dvt][:, :], AF.Square, [o_sb[dvt]], [sq])
                P.mm(p2[:, :], ones_f[:, :], sq[:, :], dvt == 0, dvt == 1, [ones_f, sq], [p2])
            msq = role("rmsq", [128, 512], F32)
            P.tt('dve', msq[:, :], mean[:, :], mean[:, :], ALU.mult, [mean], [msq])
            P.stt('dve', msq[:, :], p2[:, :], 1.0 / 256, msq[:, :], ALU.mult, ALU.subtract, [p2, msq], [msq])
            P.act(msq[:, :], msq[:, :], AF.Sqrt, [msq], [msq], bias=C.eps_t[:, 1:2])
            P.recip(msq[:, :], msq[:, :], [msq], [msq])
            yield
            for dvt, sg in enumerate((sg0, sg1)):
                t = role("rt", [128, 512], F32)
                P.tt('dve', t[:, :], o_sb[dvt][:, :], mean[:, :], ALU.subtract, [o_sb[dvt], mean], [t])
                P.tt('dve', t[:, :], t[:, :], msq[:, :], ALU.mult, [t, msq], [t])
                yb = role("ryb", [128, 512], BF16)
                P.tt('dve', yb[:, :], t[:, :], sg[:, :], ALU.mult, [t, sg], [yb])
                row = 128 * NHS + r * 256 + dvt * 128
                P.dma('sp', yT.t.ap()[row:row + 128, tok], yb[:, :], [yb], [G.y_dep[blk]])
                yield
        rdone = []

        def rbulk():
            for blk in range(NBLK):
                outs = []
                yield from rstageA(blk, outs)
                rdone.append(outs)
                yield

        rb = rbulk()
        ralive = [True]

        def rpump():
            if ralive[0]:
                try:
                    next(rb)
                except StopIteration:
                    ralive[0] = False

        for blk in range(NBLK):
            while len(rdone) <= blk:
                rpump()
            tick = 0
            for _ in rstageB(blk, rdone[blk]):
                tick += 1
                if tick % 2 == 0 and len(rdone) <= blk + 1:
                    rpump()
        while ralive[0]:
            rpump()
    P.end_phase()


NH_E = 8
NHS_O, NHR_O = 8, 4


def build_fused(T, depth):
    nc = bass.Bass("TRN2", target_bir_lowering=False)
    P = Prog(nc)
    G = Ctx()
    n_ev, n_od = (depth + 1) // 2, depth // 2
    NB = T // 512
    ext = lambda name, shape, dt=F32: P.dram(name, shape, dt, kind="ExternalInput")
    G.xT = ext("xT", [D_MODEL, T])
    G.pT = ext("pT", [depth, 256, T])
    G.gains = ext("gains", [128, KC + 3 * KC * depth])
    G.wf = {}
    G.wbf = {}
    G.wdep = {}
    for name, nt, kc in TW:
        G.wf[name] = ext(name, [depth, nt, 128, kc, 128])
        G.wbf[name] = P.dram(name + "_bf", [depth, nt, 128, kc, 128], BF16)
        for i in range(depth):
            for j in range(nt):
                G.wdep[(name, i, j)] = Dep()
    G.e_w_in = ext("e_w_in", [n_ev, 6 * NH_E, 128, 16, 128])
    G.e_w_ab_rep = ext("e_w_ab_rep", [n_ev, 2 * NH_E, 128, 16, 128])
    G.e_w_ab_tok = ext("e_w_ab_tok", [n_ev, 128, 16, 2 * NH_E])
    G.e_cvec = ext("e_cvec", [n_ev, 128, 22 * NH_E + 1])
    G.e_lru_w = ext("e_lru_w", [n_ev, 2 * NH_E, 128, 128])
    G.mwbf = {"e_w_in": P.dram("e_w_in_bf", [n_ev, 6 * NH_E, 128, 16, 128], BF16),
              "e_w_ab_rep": P.dram("e_w_ab_rep_bf", [n_ev, 2 * NH_E, 128, 16, 128], BF16)}
    G.mwf = {"e_w_in": G.e_w_in, "e_w_ab_rep": G.e_w_ab_rep}
    G.mwdep = {}
    for jj in range(n_ev):
        for t in range(6 * NH_E):
            G.mwdep[("e_w_in", jj, t)] = Dep()
        for t in range(2 * NH_E):
            G.mwdep[("e_w_ab_rep", jj, t)] = Dep()
    G.e_ctab = ext("e_ctab", [128, 7 * 128 + 512])
    G.o_w_in = ext("o_w_in", [max(n_od, 1), 3 * NHS_O + 6 * NHR_O, 128, 16, 128])
    G.o_ctab = ext("o_ctab", [128, (9 + 2 * NHR_O) * 128])
    G.o_ccol = ext("o_ccol", [128, NHS_O + 2 * NHR_O])
    G.outT = P.dram("outT", [D_MODEL, T], F32, kind="ExternalOutput")
    G.hT = P.dram("hT_s", [D_MODEL, T], F32)
    G.hnT = P.dram("hnT_s", [D_MODEL, T], BF16)
    G.yT = P.dram("yT_s", [D_MODEL, T], BF16)
    G.h_dep = [Dep() for _ in range(NB)]
    G.hn_dep = [Dep() for _ in range(NB)]
    G.y_dep = [Dep() for _ in range(NB)]

    def make_bg(i, ncalls):
        todo = [(name, j) for name, nt, kc in TW for j in range(nt)]
        per = -(-len(todo) // ncalls)

        def bg():
            for _ in range(per):
                if todo:
                    name, j = todo.pop(0)
                    P.dma('pool', G.wbf[name].t.ap()[i, j], G.wf[name].t.ap()[i, j], [], [G.wdep[(name, i, j)]])
        return bg, todo

    def make_mbg(jj, ncalls):
        todo = []
        for h in range(NH_E):
            todo += [("e_w_in", 4 * h + x) for x in range(4)] + [("e_w_ab_rep", 2 * h), ("e_w_ab_rep", 2 * h + 1)]
            todo += [("e_w_in", 4 * NH_E + 2 * h), ("e_w_in", 4 * NH_E + 2 * h + 1)]
        per = -(-len(todo) // ncalls)

        def bgm():
            for _ in range(per):
                if todo:
                    name, t = todo.pop(0)
                    P.dma('pool', G.mwbf[name].t.ap()[jj, t], G.mwf[name].t.ap()[jj, t], [], [G.mwdep[(name, jj, t)]])
        return bgm, todo

    bgm, mtodo = make_mbg(0, 4)
    emit_N(P, G, T, bgm)
    while mtodo:
        bgm()
    for i in range(depth):
        if i % 2 == 0:
            bg, todo = make_bg(i, NB)
            emit_ME(P, G, i // 2, T, NH_E, bg)
        else:
            bg, todo = make_bg(i, NB * (NHS_O + NHR_O))
            emit_MO(P, G, i // 2, T, NHS_O, NHR_O, bg)
        while todo:
            bg()
        if i % 2 == 1 and i + 1 < depth:
            bgm, mtodo = make_mbg((i + 1) // 2, NB)
            emit_T(P, G, i, T, i == depth - 1, bgm)
            while mtodo:
                bgm()
        else:
            emit_T(P, G, i, T, i == depth - 1)
    P.fence('sp')
    P.emit()
    return nc


_NC_CACHE = {}


def host_inputs(inputs, depth):
    n_ev, n_od = (depth + 1) // 2, depth // 2
    gl = [col_vec(inputs["ln_mix_w"][0])]
    for i in range(depth):
        g_next = inputs["ln_final_w"] if i == depth - 1 else inputs["ln_mix_w"][i + 1]
        gl += [col_vec(inputs["ln_mlp_w"][i]), col_vec(inputs["ln_ple_w"][i]), col_vec(g_next)]
    H = {"gains": np.ascontiguousarray(np.concatenate(gl, axis=1))}
    wout = [inputs["ev_w_out"][i // 2] if i % 2 == 0 else inputs["od_w_out"][i // 2] for i in range(depth)]
    H["w_out"] = np.stack([tile_w(w) for w in wout])
    H["w_up"] = np.stack([tile_w(inputs["w_up"][i]) for i in range(depth)])
    H["w_dn"] = np.stack([tile_w(inputs["w_down"][i]) for i in range(depth)])
    H["w_g"] = np.stack([tile_w(inputs["w_ple_gate"][i]) for i in range(depth)])
    H["w_p"] = np.stack([tile_w(inputs["w_ple_proj"][i]) for i in range(depth)])
    names = ["ev_w_in", "dn_conv_w", "dn_a_log", "dn_dt_bias", "dn_norm_w", "lru_conv_w", "lru_conv_b", "lru_wa", "lru_ba",
             "lru_wx", "lru_bx", "lru_lambda"]
    ev = [me_host_inputs(NH_E, *[inputs[n][j] for n in names]) for j in range(n_ev)]
    for k, dst in (("w_in", "e_w_in"), ("w_ab_rep", "e_w_ab_rep"), ("w_ab_tok", "e_w_ab_tok"), ("cvec", "e_cvec"),
                   ("lru_w", "e_lru_w")):
        H[dst] = np.stack([e[k] for e in ev])
    H["e_ctab"] = me_consts()
    od = [mo_host_inputs(inputs["od_w_in"][j]) for j in range(n_od)]
    H["o_w_in"] = np.stack([o["w_in"] for o in od]) if od else np.zeros((1, 3 * NHS_O + 6 * NHR_O, 128, 16, 128), np.float32)
    H["o_ctab"], H["o_ccol"] = mo_consts()
    return H


def run_fused(inputs):
    x = inputs["x"]
    B, T, D = x.shape
    depth = inputs["w_up"].shape[0]
    key = (T, depth)
    if key not in _NC_CACHE:
        _NC_CACHE[key] = build_fused(T, depth)
    nc = _NC_CACHE[key]
    H = host_inputs(inputs, depth)
    real = []
    for b in range(B):
        m = dict(H)
        m["xT"] = np.ascontiguousarray(x[b].T)
        m["pT"] = np.ascontiguousarray(np.transpose(inputs["p"][:, b], (0, 2, 1)))
        real.append(m)
    if B == 4:
        slots = [0, 1, 4, 5]
        zero = {k: np.zeros_like(v) for k, v in real[0].items()}
        maps = [zero] * 8
        maps = list(maps)
        for b, c in enumerate(slots):
            maps[c] = real[b]
    else:
        slots = list(range(B))
        maps = real
    res = run_bass_kernel_spmd(nc, maps, core_ids=list(range(len(maps))))
    out = np.empty((B, T, D), np.float32)
    for b in range(B):
        out[b] = res.results[slots[b]]["outT"].T
    return out


def kernel(x, p, ln_mix_w, ln_mlp_w, ln_ple_w, w_up, w_down, w_ple_proj, w_ple_gate, ln_final_w,
           ev_w_in, ev_w_out, dn_conv_w, dn_a_log, dn_dt_bias, dn_norm_w,
           lru_conv_w, lru_conv_b, lru_wa, lru_ba, lru_wx, lru_bx, lru_lambda, od_w_in, od_w_out):
    inputs = {k: np.asarray(v) for k, v in locals().items()}
    return run_fused(inputs)
```

```python
import numpy as np
import concourse.bass as bass
import concourse.mybir as mybir
from concourse.bass_utils import run_bass_kernel_spmd
from contextlib import ExitStack

F32 = mybir.dt.float32
BF16 = mybir.dt.bfloat16
AF = mybir.ActivationFunctionType
ALU = mybir.AluOpType

D_MODEL = 2048
KC = 16
D_FF = 8192
NORM_EPS = 1e-6
ENGS = ['pe', 'act', 'dve', 'pool', 'sp']


class Dep:
    __slots__ = ('w', 'r', 'excl')

    def __init__(self, excl=False):
        self.w = None
        self.r = {}
        self.excl = excl


class Op:
    __slots__ = ('eng', 'fn', 'waits', 'signal', 'dma', 'slot', 'val', 'sigidx', 'prev_slot_op', 'uid')


class Buf:
    def __init__(self, t, d=None):
        self.t = t
        self.d = d if d is not None else Dep()

    def __getitem__(self, idx):
        return self.t[idx]


class Pool:
    def __init__(self, bufs):
        self.bufs = bufs
        self.i = 0

    def next(self):
        b = self.bufs[self.i % len(self.bufs)]
        self.i += 1
        return b


def _dep(d):
    return d.d if isinstance(d, Buf) else d


class Prog:
    NSLOT = 8

    def __init__(self, nc):
        self.nc = nc
        self.es = ExitStack()
        self.ops = {e: [] for e in ENGS}
        self.nops = 0
        self.dma_cnt = {e: 0 for e in ENGS}
        self.slot_last = {}
        self.slot_count = {}
        self.scope = self.es
        self.phase_id = 0
        self.last_real = {}

    def begin_phase(self):
        self.phase_id += 1
        self.scope = ExitStack()

    def end_phase(self):
        lasts = dict(self.last_real)
        slots = dict(self.slot_last)
        for e in ENGS:
            o = self.op(e, None)
            for e2, p in lasts.items():
                if e2 != e:
                    p.signal = True
                    o.waits.append(p)
            for p in slots.values():
                o.waits.append(p)
        self.scope.close()
        self.scope = self.es

    def sb(self, name, shape, dtype=F32):
        name = "%s_p%d" % (name, self.phase_id)
        return Buf(self.scope.enter_context(self.nc.sbuf_tensor(name, list(shape), dtype)))

    def sbpool(self, name, shape, dtype, n):
        return Pool([self.sb("%s_%d" % (name, i), shape, dtype) for i in range(n)])

    def ps(self, name, shape, dtype=F32):
        name = "%s_p%d" % (name, self.phase_id)
        return Buf(self.scope.enter_context(self.nc.psum_tensor(name, list(shape), dtype)), Dep(excl=True))

    def dram(self, name, shape, dtype=F32, kind="Internal"):
        return Buf(self.nc.dram_tensor(name, list(shape), dtype, kind=kind))

    def op(self, eng, fn, reads=(), writes=(), dma=False):
        o = Op()
        o.eng, o.fn, o.waits, o.signal, o.dma = eng, fn, [], False, dma
        o.slot = o.val = o.sigidx = o.prev_slot_op = None
        o.uid = self.nops
        self.nops += 1
        seen = set()
        rd = [_dep(d) for d in reads]
        wr = [_dep(d) for d in writes]
        wr = wr + [d for d in rd if d.excl]
        rd = [d for d in rd if not d.excl]

        def need(p):
            if p is None or p is o or id(p) in seen:
                return
            seen.add(id(p))
            p.signal = True
            o.waits.append(p)

        def same(p):
            return p.eng == eng and not p.dma and not dma

        for d in rd:
            p = d.w
            if p is not None and not (same(p) and eng == 'pe'):
                need(p)
        for d in wr:
            p = d.w
            if p is not None and not (same(p) and not d.excl):
                if not (same(p) and eng == 'pe'):
                    need(p)
            for q in d.r.values():
                if not same(q):
                    need(q)
        for d in rd:
            d.r[eng if not dma else ('dma', o.uid)] = o
        for d in wr:
            d.w = o
            d.r = {}
        if dma:
            k = self.dma_cnt[eng]
            self.dma_cnt[eng] += 1
            o.slot = (eng, k % self.NSLOT)
            o.prev_slot_op = self.slot_last.get(o.slot)
            self.slot_count[o.slot] = self.slot_count.get(o.slot, 0) + 1
            o.val = 16 * self.slot_count[o.slot]
            self.slot_last[o.slot] = o
            o.signal = True
        self.ops[eng].append(o)
        if fn is not None and not dma:
            self.last_real[eng] = o
        return o

    def dma(self, eng, out, in_, reads=(), writes=()):
        return self.op(eng, lambda e: e.dma_start(out=out, in_=in_), reads, writes, dma=True)

    def mm(self, out, lhsT, rhs, start, stop, reads, writes):
        return self.op('pe', lambda e: e.matmul(out, lhsT=lhsT, rhs=rhs, start=start, stop=stop), reads, writes)

    def tr(self, out, in_, ident, reads, writes):
        return self.op('pe', lambda e: e.transpose(out=out, in_=in_, identity=ident), reads, writes)

    def act(self, out, in_, func, reads, writes, bias=None, scale=None):
        kw = {}
        if bias is not None:
            kw['bias'] = bias
        if scale is not None:
            kw['scale'] = scale
        return self.op('act', lambda e: e.activation(out=out, in_=in_, func=func, **kw), reads, writes)

    def tt(self, eng, out, in0, in1, op, reads, writes):
        return self.op(eng, lambda e: e.tensor_tensor(out=out, in0=in0, in1=in1, op=op), reads, writes)

    def ts(self, eng, out, in0, s1, op0, reads, writes, s2=None, op1=None):
        if op1 is None:
            return self.op(eng, lambda e: e.tensor_scalar(out=out, in0=in0, scalar1=s1, scalar2=None, op0=op0), reads, writes)
        return self.op(eng, lambda e: e.tensor_scalar(out=out, in0=in0, scalar1=s1, scalar2=s2, op0=op0, op1=op1), reads, writes)

    def stt(self, eng, out, in0, scalar, in1, op0, op1, reads, writes):
        return self.op(eng, lambda e: e.scalar_tensor_tensor(out=out, in0=in0, scalar=scalar, in1=in1, op0=op0, op1=op1),
                       reads, writes)

    def cp(self, eng, out, in_, reads, writes):
        if eng == 'act':
            return self.act(out, in_, AF.Copy, reads, writes)
        return self.op(eng, lambda e: e.tensor_copy(out=out, in_=in_), reads, writes)

    def memset(self, eng, ap, val, writes):
        return self.op(eng, lambda e: e.memset(ap, val), (), writes)

    def recip(self, out, in_, reads, writes):
        return self.op('dve', lambda e: e.reciprocal(out=out, in_=in_), reads, writes)

    def scan(self, out, d0, d1, init, reads, writes):
        return self.op('dve', lambda e: e.tensor_tensor_scan(out=out, data0=d0, data1=d1, initial=init,
                                                            op0=ALU.mult, op1=ALU.add), reads, writes)

    def fence(self, eng='sp'):
        o = self.op(eng, None)
        for p in self.slot_last.values():
            o.waits.append(p)
        return o

    def emit(self):
        nc = self.nc
        for e in ENGS:
            c = 0
            for o in self.ops[e]:
                if o.signal and not o.dma:
                    c += 1
                    o.sigidx = c
        esem = {e: self.es.enter_context(nc.semaphore("s_" + e)) for e in ENGS}
        dsem = {s: self.es.enter_context(nc.semaphore("d_%s_%d" % s)) for s in self.slot_count}

        def run(e, engine):
            waited = {}
            for o in self.ops[e]:
                ws = list(o.waits)
                if o.dma and o.prev_slot_op is not None:
                    ws.append(o.prev_slot_op)
                for p in ws:
                    if p.dma:
                        key, sem, val = p.slot, dsem[p.slot], p.val
                    else:
                        key, sem, val = p.eng, esem[p.eng], p.sigidx
                    if waited.get(key, 0) >= val:
                        continue
                    waited[key] = val
                    engine.wait_ge(sem, val)
                if o.fn is None:
                    continue
                ins = o.fn(engine)
                if o.dma:
                    ins.then_inc(dsem[o.slot], 16)
                elif o.signal:
                    ins.then_inc(esem[e], 1)

        with nc.Block() as block:
            if self.ops['sp']:
                block.sync(lambda eng: run('sp', eng))
            if self.ops['pe']:
                block.tensor(lambda eng: run('pe', eng))
            if self.ops['act']:
                block.scalar(lambda eng: run('act', eng))
            if self.ops['dve']:
                block.vector(lambda eng: run('dve', eng))
            if self.ops['pool']:
                block.gpsimd(lambda eng: run('pool', eng))
        self.es.close()


def tile_w(W):
    K, N = W.shape
    return np.ascontiguousarray(W.reshape(K // 128, 128, N // 128, 128).transpose(2, 1, 0, 3))


def col_vec(v):
    return np.ascontiguousarray(v.reshape(-1, 128).T)


def fm(ap, p=128):
    return ap.rearrange("(kc p) t -> p kc t", p=p)


class Ctx:
    pass


def emit_rmsnorm(P, C, h, gbuf, goff, out, nchunk=KC, n=512, eps=NORM_EPS, dim=D_MODEL):
    ps = C.pstat.next()
    for kc in range(nchunk):
        sq = C.sq.next()
        P.act(sq[:, :n], h[:, kc, :n], AF.Square, [h], [sq])
        P.mm(ps[:, :n], C.ones_bf[:, :], sq[:, :n], kc == 0, kc == nchunk - 1, [sq, C.ones_bf], [ps])
    rs = C.rstd.next()
    P.act(rs[:, :n], ps[:, :n], AF.Ln, [ps], [rs], bias=C.eps_t[:, 0:1], scale=1.0 / dim)
    P.act(rs[:, :n], rs[:, :n], AF.Exp, [rs], [rs], scale=-0.5)
    for kc in range(nchunk):
        P.stt('dve', out[:, kc, :n], h[:, kc, :n], gbuf[:, goff + kc:goff + kc + 1], rs[:, :n],
              ALU.mult, ALU.mult, [h, rs, gbuf], [out])


def load_consts(P, C, need_eps=(NORM_EPS,)):
    C.ones_bf = P.sb("ones_bf", [128, 128], BF16)
    P.memset('pool', C.ones_bf[:, :], 1.0, [C.ones_bf])
    C.eps_t = P.sb("eps_t", [128, 4])
    P.memset('pool', C.eps_t[:, 0:1], NORM_EPS, [C.eps_t])
    P.memset('pool', C.eps_t[:, 1:2], 1e-5, [C.eps_t])
    P.memset('pool', C.eps_t[:, 2:3], 1.0, [C.eps_t])
    P.memset('pool', C.eps_t[:, 3:4], 0.0, [C.eps_t])


def emit_N(P, G, T, bgm=lambda: None):
    P.begin_phase()
    C = Ctx()
    load_consts(P, C)
    C.pstat = Pool([P.ps("pstat%d" % i, [128, 512]) for i in range(2)])
    C.sq = P.sbpool("sq", [128, 512], BF16, 3)
    C.rstd = P.sbpool("rstd", [128, 512], F32, 2)
    g = P.sb("g", [128, KC])
    P.dma('sp', g[:, :], G.gains.t.ap()[:, 0:KC], [], [g])
    hp = P.sbpool("h", [128, KC, 512], F32, 2)
    op = P.sbpool("o", [128, KC, 512], BF16, 2)
    for blk in range(T // 512):
        tok = slice(blk * 512, (blk + 1) * 512)
        h = hp.next()
        P.dma('sp', h[:, :, :], fm(G.xT.t.ap())[:, :, tok], [], [h])
        bgm()
        o = op.next()
        emit_rmsnorm(P, C, h, g, 0, o)
        P.dma('sp', fm(G.hnT.t.ap())[:, :, tok], o[:, :, :], [o], [G.hn_dep[blk]])
    P.end_phase()


TW = (("w_out", 16, 16), ("w_up", 64, 16), ("w_dn", 16, 64), ("w_g", 16, 16), ("w_p", 16, 2))


def emit_T(P, G, i, T, last, bgm=lambda: None):
    P.begin_phase()
    C = Ctx()
    load_consts(P, C)
    hsrc = G.xT if i == 0 else G.hT
    C.pstat = Pool([P.ps("pstat%d" % k, [128, 512]) for k in range(2)])
    pmm = Pool([P.ps("pmm%d" % k, [128, 512]) for k in range(6)])
    C.sq = P.sbpool("sq", [128, 512], BF16, 2)
    C.rstd = P.sbpool("rstd", [128, 512], F32, 1)
    g = P.sb("g", [128, 3 * KC])
    P.dma('sp', g[:, :], G.gains.t.ap()[:, KC + 3 * KC * i:KC + 3 * KC * (i + 1)], [], [g])
    hp = P.sbpool("h", [128, KC, 512], F32, 2)
    hnp = P.sbpool("hn", [128, KC, 512], BF16, 1)
    ap_ = P.sbpool("a", [128, 64, 512], BF16, 1)
    pp = P.sbpool("p", [128, 2, 512], BF16, 2)
    w16 = P.sbpool("w16", [128, 16, 128], BF16, 4)
    w64 = P.sbpool("w64", [128, 64, 128], BF16, 2)
    w2 = P.sbpool("w2", [128, 2, 128], BF16, 2)
    rt = P.sbpool("rt", [128, 512], F32, 2)
    NBLK = T // 512
    a = ap_.next()
    hs, pbs = {}, {}

    def load_hp(b):
        tk = slice(b * 512, (b + 1) * 512)
        hs[b] = hp.next()
        P.dma('sp', hs[b][:, :, :], fm(hsrc.t.ap())[:, :, tk], [G.h_dep[b]], [hs[b]])
        pbs[b] = pp.next()
        P.dma('pool', pbs[b][:, :, :], fm(G.pT.t.ap()[i])[:, :, tk], [], [pbs[b]])

    def load_y(b):
        tk = slice(b * 512, (b + 1) * 512)
        P.dma('sp', a[:, 0:KC, :], fm(G.yT.t.ap())[:, :, tk], [G.y_dep[b]], [a])

    load_hp(0)
    load_y(0)

    def wload(name, j, w):
        P.dma('sp', w[:, :, :], G.wbf[name].t.ap()[i, j], [G.wdep[(name, i, j)]], [w])

    for blk in range(T // 512):
        tok = slice(blk * 512, (blk + 1) * 512)
        h = hs.pop(blk)
        pb = pbs.pop(blk)
        bgm()
        for j in range(16):
            w = w16.next()
            wload("w_out", j, w)
            ps = pmm.next()
            for kc in range(KC):
                P.mm(ps[:, :], w[:, kc, :], a[:, kc, :], kc == 0, kc == KC - 1, [w, a], [ps])
            P.tt('dve', h[:, j, :], h[:, j, :], ps[:, :], ALU.add, [h, ps], [h])
        if blk + 1 < NBLK:
            load_hp(blk + 1)
        hn = hnp.next()
        emit_rmsnorm(P, C, h, g, 0, hn)
        for j in range(64):
            w = w16.next()
            wload("w_up", j, w)
            ps = pmm.next()
            for kc in range(KC):
                P.mm(ps[:, :], w[:, kc, :], hn[:, kc, :], kc == 0, kc == KC - 1, [w, hn], [ps])
            r = rt.next()
            P.act(r[:, :], ps[:, :], AF.Relu, [ps], [r])
            P.tt('dve', a[:, j, :], r[:, :], r[:, :], ALU.mult, [r], [a])
        for j in range(16):
            w = w64.next()
            wload("w_dn", j, w)
            ps = pmm.next()
            for kc in range(64):
                P.mm(ps[:, :], w[:, kc, :], a[:, kc, :], kc == 0, kc == 63, [w, a], [ps])
            P.tt('dve', h[:, j, :], h[:, j, :], ps[:, :], ALU.add, [h, ps], [h])
        if blk + 1 < NBLK:
            load_y(blk + 1)
        emit_rmsnorm(P, C, h, g, KC, hn)
        for j in range(16):
            w = w16.next()
            wload("w_g", j, w)
            wp = w2.next()
            wload("w_p", j, wp)
            ps = pmm.next()
            for kc in range(KC):
                P.mm(ps[:, :], w[:, kc, :], hn[:, kc, :], kc == 0, kc == KC - 1, [w, hn], [ps])
            gt = rt.next()
            P.act(gt[:, :], ps[:, :], AF.Sigmoid, [ps], [gt])
            ps2 = pmm.next()
            for kc in range(2):
                P.mm(ps2[:, :], wp[:, kc, :], pb[:, kc, :], kc == 0, kc == 1, [wp, pb], [ps2])
            P.tt('dve', gt[:, :], gt[:, :], ps2[:, :], ALU.mult, [gt, ps2], [gt])
            P.tt('pool', h[:, j, :], h[:, j, :], gt[:, :], ALU.add, [h, gt], [h])
        o = h if last else hn
        emit_rmsnorm(P, C, h, g, 2 * KC, o)
        if last:
            P.dma('sp', fm(G.outT.t.ap())[:, :, tok], o[:, :, :], [o], [Dep()])
        else:
            P.dma('sp', fm(G.hT.t.ap())[:, :, tok], h[:, :, :], [h], [G.h_dep[blk]])
            P.dma('sp', fm(G.hnT.t.ap())[:, :, tok], o[:, :, :], [o], [G.hn_dep[blk]])
    P.end_phase()


CH = 64


def me_consts():
    i = np.arange(128)
    blk = i // CH
    same = blk[:, None] == blk[None, :]
    ident = np.eye(128)
    U = ((i[:, None] <= i[None, :]) & same)
    Lgt = (i[:, None] > i[None, :])
    ones_bd = same
    m_strict = ((i[:, None] > i[None, :]) & same)
    mT_strict = ((i[:, None] < i[None, :]) & same)
    mT_incl = ((i[:, None] <= i[None, :]) & same)
    reset = (np.arange(512) % CH != 0)
    tab = np.concatenate([ident, U, Lgt, ones_bd, m_strict, mT_strict, mT_incl,
                          np.broadcast_to(reset[None, :], (128, 512))], axis=1)
    return np.ascontiguousarray(tab.astype(np.float32))


def emit_ME(P, G, j, T, NH, bg):
    P.begin_phase()
    C = Ctx()
    NSEG = T // 512
    hnT, yT = G.hnT, G.yT
    w_in = ('e_w_in', G.mwbf['e_w_in'].t.ap()[j])
    w_ab_rep = ('e_w_ab_rep', G.mwbf['e_w_ab_rep'].t.ap()[j])
    w_ab_tok = G.e_w_ab_tok.t.ap()[j]
    cvec_d = G.e_cvec.t.ap()[j]
    lru_w = G.e_lru_w.t.ap()[j]
    ctab_d = G.e_ctab.t.ap()
    o_lcw, o_lcb, o_ba, o_bx, o_lam, o_nw, o_alog, o_dtb = 12 * NH, 16 * NH, 17 * NH, 18 * NH, 19 * NH, 20 * NH, 20 * NH + 1, 21 * NH + 1
    load_consts(P, C)
    ctab = P.sb("ctab_sb", [128, 7 * 128 + 512])
    P.dma('sp', ctab[:, :], ctab_d, [], [ctab])
    ident, U, Lgt, onesbd, m_strict, mT_strict, mT_incl = [ctab[:, k * 128:(k + 1) * 128] for k in range(7)]
    reset = ctab[:, 7 * 128:7 * 128 + 512]
    cv = P.sb("cv_sb", [128, 22 * NH + 1])
    P.dma('sp', cv[:, :], cvec_d, [], [cv])
    wab = P.sb("wab", [128, 16, 2 * NH], BF16)
    P.dma('pool', wab[:, :, :], w_ab_tok, [], [wab])
    lw = P.sb("lw", [128, 2 * NH, 128], BF16)
    for k in range(2 * NH):
        P.dma('pool', lw[:, k, :], lru_w[k], [], [lw])
    der = P.sb("der", [128, 3 * NH])
    P.act(der[:, 0:NH], cv[:, o_alog:o_alog + NH], AF.Exp, [cv], [der])
    P.ts('dve', der[:, 0:NH], der[:, 0:NH], -1.0, ALU.mult, [der], [der])
    P.act(der[:, NH:2 * NH], cv[:, o_lam:o_lam + NH], AF.Exp, [cv], [der], scale=-1.0)
    P.act(der[:, NH:2 * NH], der[:, NH:2 * NH], AF.Ln, [der], [der], bias=C.eps_t[:, 2:3])
    P.ts('dve', der[:, 2 * NH:3 * NH], der[:, NH:2 * NH], -16.0, ALU.mult, [der], [der])
    P.ts('dve', der[:, NH:2 * NH], der[:, NH:2 * NH], -8.0, ALU.mult, [der], [der])
    dtb_t = P.sb("dtb_t", [128, 4, NH])
    negA_t = P.sb("negA_t", [128, 4, NH])
    for p_ in range(4):
        P.cp('dve', dtb_t[:, p_, :], cv[:, o_dtb:o_dtb + NH], [cv], [dtb_t])
        P.cp('dve', negA_t[:, p_, :], der[:, 0:NH], [der], [negA_t])
    mask4 = P.sb("mask4", [128, 4 * 512])
    for k_, src in enumerate((ident, m_strict, mT_strict, mT_incl)):
        for p_ in range(4):
            P.cp('dve', mask4[:, k_ * 512 + p_ * 128:k_ * 512 + (p_ + 1) * 128], src, [ctab], [mask4])
    tails = P.sb("tails", [128, 4 * NH, 3])
    P.memset('dve', tails[:, :, :], 0.0, [tails])
    S = [P.sb("S%d" % h, [128, 128]) for h in range(NH)]
    Sb = [P.sb("Sb%d" % h, [128, 128], BF16) for h in range(NH)]
    for h in range(NH):
        P.memset('dve', S[h][:, :], 0.0, [S[h]])
        P.memset('dve', Sb[h][:, :], 0.0, [Sb[h]])
    hprev = P.sb("hprev", [128, NH])
    P.memset('dve', hprev[:, :], 0.0, [hprev])
    pmm = Pool([P.ps("pmm%d" % i, [128, 512]) for i in range(2)])
    C.pstat = Pool([P.ps("pstat%d" % i, [128, 512]) for i in range(1)])
    psm = Pool([P.ps("psm%d" % i, [128, 512]) for i in range(5)])
    C.sq = P.sbpool("sq", [128, 512], BF16, 2)
    hnp = P.sbpool("hn", [128, KC, 512], BF16, 2)
    w16 = P.sbpool("w16", [128, 16, 128], BF16, 6)
    bigA = P.sbpool("bigA", [128, 515], F32, 10)
    bigL = P.sbpool("bigL", [128, 515], F32, 8)
    bigB = P.sbpool("bigB", [128, 515], F32, 2)
    pools = {}

    def role(name, shape, dtype, n=2):
        if name not in pools:
            pools[name] = P.sbpool(name, shape, dtype, n)
        return pools[name].next()

    def inproj(wd, idx, hn):
        w = w16.next()
        wname, wap = wd
        P.dma('sp', w[:, :, :], wap[idx], [G.mwdep[(wname, j, idx)]], [w])
        ps = pmm.next()
        for kc in range(KC):
            P.mm(ps[:, :], w[:, kc, :], hn[:, kc, :], kc == 0, kc == KC - 1, [w, hn], [ps])
            if kc % 4 == 3 and kc != KC - 1:
                yield
        return ps

    def conv(ps, tidx, cwoff, big, bias_col=None):
        pre = big.next()
        P.cp('dve', pre[:, 0:3], tails[:, tidx, :], [tails], [pre])
        P.cp('act', pre[:, 3:515], ps[:, :], [ps], [pre])
        yield
        acc = big.next()
        if bias_col is None:
            P.ts('dve', acc[:, 0:512], pre[:, 0:512], cv[:, cwoff:cwoff + 1], ALU.mult, [pre, cv], [acc])
        else:
            P.ts('dve', acc[:, 0:512], pre[:, 0:512], cv[:, cwoff:cwoff + 1], ALU.mult, [pre, cv], [acc],
                 s2=cv[:, bias_col:bias_col + 1], op1=ALU.add)
        for j in range(1, 4):
            yield
            P.stt('dve', acc[:, 0:512], pre[:, j:j + 512], cv[:, cwoff + j:cwoff + j + 1], acc[:, 0:512], ALU.mult, ALU.add,
                  [pre, cv, acc], [acc])
        P.cp('dve', tails[:, tidx, :], pre[:, 512:515], [pre], [tails])
        yield
        return acc

    def bcast_sumsq(x):
        sq = C.sq.next()
        P.act(sq[:, :], x[:, 0:512], AF.Square, [x], [sq])
        ps = C.pstat.next()
        P.mm(ps[:, :], C.ones_bf[:, :], sq[:, :], True, True, [sq, C.ones_bf], [ps])
        return ps

    def seg_prologue(seg):
        sc = Ctx()
        sc.seg = seg
        sc.tok = slice(seg * 512, (seg + 1) * 512)
        hn = sc.hn = hnp.next()
        P.dma('sp', hn[:, :, :], fm(hnT.t.ap())[:, :, sc.tok], [G.hn_dep[seg]], [hn])
        bg()
        psr = psm.next()
        for p_ in range(4):
            for kc in range(KC):
                P.mm(psr[:, p_ * 2 * NH:(p_ + 1) * 2 * NH], hn[:, kc, p_ * 128:(p_ + 1) * 128], wab[:, kc, :], kc == 0, kc == KC - 1,
                     [hn, wab], [psr])
        raw = role("raw", [128, 4, 2 * NH], F32)
        P.cp('dve', raw[:, :, :], psr[:, 0:8 * NH].rearrange("p (a b) -> p a b", b=2 * NH), [psr], [raw])
        beta_tok = sc.beta_tok = role("beta_tok", [128, 4, NH], F32)
        P.act(beta_tok[:, :, :], raw[:, :, 0:NH], AF.Sigmoid, [raw], [beta_tok])
        xs = role("xs", [128, 4, NH], F32)
        P.tt('dve', xs[:, :, :], raw[:, :, NH:2 * NH], dtb_t[:, :, :], ALU.add, [raw, dtb_t], [xs])
        ax = role("ax", [128, 4, NH], F32)
        P.act(ax[:, :, :], xs[:, :, :], AF.Abs, [xs], [ax])
        P.act(ax[:, :, :], ax[:, :, :], AF.Exp, [ax], [ax], scale=-1.0)
        P.act(ax[:, :, :], ax[:, :, :], AF.Ln, [ax], [ax], bias=C.eps_t[:, 2:3])
        P.ts('dve', xs[:, :, :], xs[:, :, :], 0.0, ALU.max, [xs], [xs])
        P.tt('dve', xs[:, :, :], xs[:, :, :], ax[:, :, :], ALU.add, [xs, ax], [xs])
        g_tok = sc.g_tok = role("g_tok", [128, 4, NH], F32)
        P.tt('dve', g_tok[:, :, :], xs[:, :, :], negA_t[:, :, :], ALU.mult, [xs, negA_t], [g_tok])
        psc = psm.next()
        for p_ in range(4):
            P.mm(psc[:, p_ * NH:(p_ + 1) * NH], U, g_tok[:, p_, :], True, True, [ctab, g_tok], [psc])
            P.mm(psc[:, 4 * NH + p_ * NH:4 * NH + (p_ + 1) * NH], onesbd, g_tok[:, p_, :], True, True, [ctab, g_tok], [psc])
        gc_tok = role("gc_tok", [128, 8 * NH], F32)
        P.cp('dve', gc_tok[:, :], psc[:, 0:8 * NH], [psc], [gc_tok])
        egc_tok = role("egc_tok", [128, 4 * NH], F32)
        P.act(egc_tok[:, :], gc_tok[:, 0:4 * NH], AF.Exp, [gc_tok], [egc_tok])
        bg_tok = sc.bg_tok = role("bg_tok", [128, 4 * NH], F32)
        P.tt('dve', bg_tok[:, :], egc_tok[:, :], beta_tok[:, :, :].rearrange("p a b -> p (a b)"), ALU.mult,
             [egc_tok, beta_tok], [bg_tok])
        dec_tok = sc.dec_tok = role("dec_tok", [128, 4 * NH], F32)
        P.tt('dve', dec_tok[:, :], gc_tok[:, 4 * NH:8 * NH], gc_tok[:, 0:4 * NH], ALU.subtract, [gc_tok], [dec_tok])
        P.act(dec_tok[:, :], dec_tok[:, :], AF.Exp, [dec_tok], [dec_tok])
        return sc

    def stageA(sc, h, a):
        hn = sc.hn
        ps = yield from inproj(w_in, 4 * h + 0, hn)
        cq = yield from conv(ps, 3 * h + 0, 4 * (3 * h + 0), bigA)
        yield
        P.act(cq[:, 0:512], cq[:, 0:512], AF.Silu, [cq], [cq])
        yield
        ps = bcast_sumsq(cq)
        rn = bigA.next()
        P.act(rn[:, 0:512], ps[:, :], AF.Ln, [ps], [rn], bias=C.eps_t[:, 0:1])
        yield
        P.act(rn[:, 0:512], rn[:, 0:512], AF.Exp, [rn], [rn], scale=-0.5)
        yield
        qn = role("qn", [128, 512], F32)
        P.stt('dve', qn[:, :], cq[:, 0:512], 128.0 ** -0.5, rn[:, 0:512], ALU.mult, ALU.mult, [cq, rn], [qn])
        yield
        q_bf = a['q_bf'] = role("q_bf", [128, 512], BF16)
        P.cp('act', q_bf[:, :], qn[:, :], [qn], [q_bf])
        yield
        ps = yield from inproj(w_in, 4 * h + 1, hn)
        ck = yield from conv(ps, 3 * h + 1, 4 * (3 * h + 1), bigA)
        yield
        P.act(ck[:, 0:512], ck[:, 0:512], AF.Silu, [ck], [ck])
        yield
        ps = bcast_sumsq(ck)
        rn2 = bigA.next()
        P.act(rn2[:, 0:512], ps[:, :], AF.Ln, [ps], [rn2], bias=C.eps_t[:, 0:1])
        yield
        P.act(rn2[:, 0:512], rn2[:, 0:512], AF.Exp, [rn2], [rn2], scale=-0.5)
        yield
        kn = a['kn'] = role("kn", [128, 512], F32)
        P.tt('dve', kn[:, :], ck[:, 0:512], rn2[:, 0:512], ALU.mult, [ck, rn2], [kn])
        yield
        k_bf = a['k_bf'] = role("k_bf", [128, 512], BF16)
        P.cp('act', k_bf[:, :], kn[:, :], [kn], [k_bf])
        yield
        ps = yield from inproj(w_in, 4 * h + 2, hn)
        cvv = yield from conv(ps, 3 * h + 2, 4 * (3 * h + 2), bigA)
        c_v = a['c_v'] = role("c_v", [128, 512], F32)
        P.act(c_v[:, :], cvv[:, 0:512], AF.Silu, [cvv], [c_v])
        yield
        sz = a['sz'] = role("sz", [128, 512], F32)
        ps = yield from inproj(w_in, 4 * h + 3, hn)
        P.act(sz[:, :], ps[:, :], AF.Silu, [ps], [sz])
        yield
        beta_b = bigA.next()
        ps = yield from inproj(w_ab_rep, 2 * h, hn)
        P.act(beta_b[:, 0:512], ps[:, :], AF.Sigmoid, [ps], [beta_b])
        kb_bf = a['kb_bf'] = role("kb_bf", [128, 512], BF16)
        P.tt('dve', kb_bf[:, :], kn[:, :], beta_b[:, 0:512], ALU.mult, [kn, beta_b], [kb_bf])
        yield
        ps = yield from inproj(w_ab_rep, 2 * h + 1, hn)
        t1 = bigA.next()
        P.act(t1[:, 0:512], ps[:, :], AF.Abs, [ps, cv], [t1], bias=cv[:, o_dtb + h:o_dtb + h + 1])
        t4 = bigA.next()
        P.ts('dve', t4[:, 0:512], ps[:, :], cv[:, o_dtb + h:o_dtb + h + 1], ALU.add, [ps, cv], [t4], s2=0.0, op1=ALU.max)
        yield
        P.act(t1[:, 0:512], t1[:, 0:512], AF.Exp, [t1], [t1], scale=-1.0)
        P.act(t1[:, 0:512], t1[:, 0:512], AF.Ln, [t1], [t1], bias=C.eps_t[:, 2:3])
        yield
        P.tt('dve', t1[:, 0:512], t1[:, 0:512], t4[:, 0:512], ALU.add, [t1, t4], [t1])
        P.ts('dve', t1[:, 0:512], t1[:, 0:512], der[:, h:h + 1], ALU.mult, [t1, der], [t1])
        yield
        gc_b = bigA.next()
        P.scan(gc_b[:, 0:512], reset, t1[:, 0:512], 0.0, [ctab, t1], [gc_b])
        yield
        egc_b = a['egc_b'] = role("egc_b", [128, 512], F32)
        P.act(egc_b[:, :], gc_b[:, 0:512], AF.Exp, [gc_b], [egc_b])
        qd_bf = a['qd_bf'] = role("qd_bf", [128, 512], BF16)
        P.tt('dve', qd_bf[:, :], qn[:, :], egc_b[:, :], ALU.mult, [qn, egc_b], [qd_bf])
        yield

    def stageL(sc, g):
        hn = sc.hn
        ps = yield from inproj(w_in, 4 * NH + 2 * g, hn)
        xc = yield from conv(ps, 3 * NH + g, o_lcw + 4 * g, bigL, bias_col=o_lcb + g)
        yield
        ps = yield from inproj(w_in, 4 * NH + 2 * g + 1, hn)
        gy = role("gy", [128, 512], F32)
        P.act(gy[:, :], ps[:, :], AF.Gelu, [ps], [gy])
        xc_bf = role("xc_bf", [128, 512], BF16)
        P.cp('act', xc_bf[:, :], xc[:, 0:512], [xc], [xc_bf])
        yield
        ps = pmm.next()
        P.mm(ps[:, :], lw[:, g, :], xc_bf[:, :], True, True, [lw, xc_bf], [ps])
        rg = bigL.next()
        P.act(rg[:, 0:512], ps[:, :], AF.Sigmoid, [ps, cv], [rg], bias=cv[:, o_ba + g:o_ba + g + 1])
        ps = pmm.next()
        P.mm(ps[:, :], lw[:, NH + g, :], xc_bf[:, :], True, True, [lw, xc_bf], [ps])
        ig = bigL.next()
        P.act(ig[:, 0:512], ps[:, :], AF.Sigmoid, [ps, cv], [ig], bias=cv[:, o_bx + g:o_bx + g + 1])
        yield
        aa = bigL.next()
        P.act(aa[:, 0:512], rg[:, 0:512], AF.Exp, [rg, der], [aa], scale=der[:, NH + g:NH + g + 1])
        a2 = bigL.next()
        P.act(a2[:, 0:512], rg[:, 0:512], AF.Exp, [rg, der], [a2], scale=der[:, 2 * NH + g:2 * NH + g + 1])
        P.act(a2[:, 0:512], a2[:, 0:512], AF.Sqrt, [a2], [a2], bias=C.eps_t[:, 2:3], scale=-1.0)
        yield
        P.tt('dve', ig[:, 0:512], ig[:, 0:512], xc[:, 0:512], ALU.mult, [ig, xc], [ig])
        P.tt('dve', ig[:, 0:512], ig[:, 0:512], a2[:, 0:512], ALU.mult, [ig, a2], [ig])
        yield
        hs = bigL.next()
        P.scan(hs[:, 0:512], aa[:, 0:512], ig[:, 0:512], hprev[:, g:g + 1], [aa, ig, hprev], [hs])
        P.cp('dve', hprev[:, g:g + 1], hs[:, 511:512], [hs], [hprev])
        yield
        y_bf = role("yl_bf", [128, 512], BF16)
        P.tt('dve', y_bf[:, :], hs[:, 0:512], gy[:, :], ALU.mult, [hs, gy], [y_bf])
        P.dma('pool', yT.t.ap()[128 * NH + g * 128:128 * NH + (g + 1) * 128, sc.tok], y_bf[:, :], [y_bf], [G.y_dep[sc.seg]])
        yield

    def stageB(sc, h, a):
        beta_tok, g_tok, bg_tok, dec_tok = sc.beta_tok, sc.g_tok, sc.bg_tok, sc.dec_tok
        c_v, kn, k_bf, kb_bf, q_bf, qd_bf, egc_b, sz = (a[k] for k in ('c_v', 'kn', 'k_bf', 'kb_bf', 'q_bf', 'qd_bf', 'egc_b', 'sz'))
        o_sb = role("o_sb", [128, 512], F32)
        cs = [slice(p_ * 128, (p_ + 1) * 128) for p_ in range(4)]
        bV4 = role("bV4", [128, 512], F32, 1)
        bkd4 = role("bkd4", [128, 512], F32, 1)
        kdec4 = role("kdec4", [128, 512], BF16, 1)
        gU4 = role("gU4", [128, 512], F32, 1)
        pst = psm.next()
        for p_ in range(4):
            P.tr(pst[:, cs[p_]], c_v[:, cs[p_]], ident, [c_v, ctab], [pst])
        for p_ in range(4):
            P.ts('dve', bV4[:, cs[p_]], pst[:, cs[p_]], beta_tok[:, p_, h:h + 1], ALU.mult, [pst, beta_tok], [bV4])
        yield
        pst = psm.next()
        for p_ in range(4):
            P.tr(pst[:, cs[p_]], kn[:, cs[p_]], ident, [kn, ctab], [pst])
        for p_ in range(4):
            col = p_ * NH + h
            P.ts('dve', bkd4[:, cs[p_]], pst[:, cs[p_]], bg_tok[:, col:col + 1], ALU.mult, [pst, bg_tok], [bkd4])
            P.act(kdec4[:, cs[p_]], pst[:, cs[p_]], AF.Copy, [pst, dec_tok], [kdec4], scale=dec_tok[:, col:col + 1])
        yield
        for p_ in range(4):
            P.ts('dve', gU4[:, cs[p_]], U, g_tok[:, p_, h:h + 1], ALU.mult, [ctab, g_tok], [gU4])
        psd = psm.next()
        for p_ in range(4):
            P.mm(psd[:, cs[p_]], gU4[:, cs[p_]], Lgt, True, True, [gU4, ctab], [psd])
        Dm4 = role("Dm4", [128, 512], F32, 1)
        P.act(Dm4[:, :], psd[:, :], AF.Exp, [psd], [Dm4])
        yield
        psd = psm.next()
        for p_ in range(4):
            P.mm(psd[:, cs[p_]], Lgt, gU4[:, cs[p_]], True, True, [gU4, ctab], [psd])
        DmT4 = role("DmT4", [128, 512], F32, 1)
        P.act(DmT4[:, :], psd[:, :], AF.Exp, [psd], [DmT4])
        yield
        psM = psm.next()
        for p_ in range(4):
            P.mm(psM[:, cs[p_]], kb_bf[:, cs[p_]], k_bf[:, cs[p_]], True, True, [kb_bf, k_bf], [psM])
        tq = role("tq4", [128, 512], F32, 2)
        P.tt('dve', tq[:, :], psM[:, :], Dm4[:, :], ALU.mult, [psM, Dm4], [tq])
        QT = role("QT4", [128, 512], F32, 2)
        P.stt('dve', QT[:, :], tq[:, :], -1.0, mask4[:, 512:1024], ALU.mult, ALU.mult, [tq, mask4], [QT])
        yield
        psM = psm.next()
        for p_ in range(4):
            P.mm(psM[:, cs[p_]], k_bf[:, cs[p_]], kb_bf[:, cs[p_]], True, True, [kb_bf, k_bf], [psM])
        tq = role("tq4", [128, 512], F32, 2)
        P.tt('dve', tq[:, :], psM[:, :], DmT4[:, :], ALU.mult, [psM, DmT4], [tq])
        Q = role("Q4", [128, 512], F32, 2)
        P.stt('dve', Q[:, :], tq[:, :], -1.0, mask4[:, 1024:1536], ALU.mult, ALU.mult, [tq, mask4], [Q])
        R = role("R4", [128, 512], F32, 2)
        P.tt('dve', R[:, :], Q[:, :], mask4[:, 0:512], ALU.add, [Q, mask4], [R])
        yield
        psA = psm.next()
        for p_ in range(4):
            P.mm(psA[:, cs[p_]], k_bf[:, cs[p_]], q_bf[:, cs[p_]], True, True, [k_bf, q_bf], [psA])
        tq = role("tq4", [128, 512], F32, 2)
        P.tt('dve', tq[:, :], psA[:, :], DmT4[:, :], ALU.mult, [psA, DmT4], [tq])
        AT4 = role("AT4", [128, 512], BF16, 1)
        P.tt('dve', AT4[:, :], tq[:, :], mask4[:, 1536:2048], ALU.mult, [tq, mask4], [AT4])
        yield
        for lvl in range(1, 6):
            psq = psm.next()
            for p_ in range(4):
                P.mm(psq[:, cs[p_]], Q[:, cs[p_]], QT[:, cs[p_]], True, True, [Q, QT], [psq])
            QTn = role("QT4", [128, 512], F32, 2)
            P.cp('act', QTn[:, :], psq[:, :], [psq], [QTn])
            yield
            if lvl < 5:
                psq = psm.next()
                for p_ in range(4):
                    P.mm(psq[:, cs[p_]], QT[:, cs[p_]], Q[:, cs[p_]], True, True, [Q, QT], [psq])
                Qn = role("Q4", [128, 512], F32, 2)
                P.cp('dve', Qn[:, :], psq[:, :], [psq], [Qn])
                Q = Qn
                yield
            QT = QTn
            psq = psm.next()
            for p_ in range(4):
                P.mm(psq[:, cs[p_]], QT[:, cs[p_]], R[:, cs[p_]], True, True, [QT, R], [psq])
            Rn = role("R4", [128, 512], F32, 2)
            P.tt('dve', Rn[:, :], psq[:, :], R[:, :], ALU.add, [psq, R], [Rn])
            R = Rn
            yield
        psu = psm.next()
        for p_ in range(4):
            P.mm(psu[:, cs[p_]], R[:, cs[p_]], bV4[:, cs[p_]], True, True, [R, bV4], [psu])
        u4 = role("u4", [128, 512], F32, 1)
        P.cp('act', u4[:, :], psu[:, :], [psu], [u4])
        yield
        psw = psm.next()
        for p_ in range(4):
            P.mm(psw[:, cs[p_]], bkd4[:, cs[p_]], R[:, cs[p_]], True, True, [R, bkd4], [psw])
        wT4 = role("wT4", [128, 512], BF16, 1)
        P.cp('dve', wT4[:, :], psw[:, :], [psw], [wT4])
        yield
        prs = []
        for p_ in range(4):
            pr = Ctx()
            pr.wT, pr.u_sb, pr.kdec, pr.AT = wT4, u4, kdec4, AT4
            pr.c0 = p_ * 128
            prs.append(pr)
        for p_, pr in enumerate(prs):
            vn = role("vn", [128, 128], BF16)
            for c in range(2):
                r0 = slice(c * CH, (c + 1) * CH)
                tc_ = slice(p_ * 128 + c * CH, p_ * 128 + (c + 1) * CH)
                last = p_ * 128 + (c + 1) * CH - 1
                psws = psm.next()
                P.mm(psws[:, 0:128], pr.wT[:, pr.c0:pr.c0 + 128], Sb[h][:, :], True, True, [pr.wT, Sb[h]], [psws])
                P.tt('dve', vn[r0, :], pr.u_sb[r0, pr.c0:pr.c0 + 128], psws[r0, 0:128], ALU.subtract, [pr.u_sb, psws], [vn])
                yield
                pskv = psm.next()
                P.mm(pskv[:, 0:128], pr.kdec[r0, pr.c0:pr.c0 + 128], vn[r0, :], True, True, [pr.kdec, vn], [pskv])
                pso = psm.next()
                P.mm(pso[:, 0:CH], Sb[h][:, :], qd_bf[:, tc_], True, False, [Sb[h], qd_bf], [pso])
                P.mm(pso[:, 0:CH], vn[r0, :], pr.AT[r0, pr.c0 + c * CH:pr.c0 + (c + 1) * CH], False, True, [vn, pr.AT], [pso])
                P.stt('dve', Sb[h][:, :], S[h][:, :], egc_b[:, last:last + 1], pskv[:, 0:128], ALU.mult, ALU.add,
                      [S[h], egc_b, pskv], [Sb[h]])
                P.stt('dve', S[h][:, :], S[h][:, :], egc_b[:, last:last + 1], pskv[:, 0:128], ALU.mult, ALU.add,
                      [S[h], egc_b, pskv], [S[h]])
                P.cp('act', o_sb[:, tc_], pso[:, 0:CH], [pso], [o_sb])
                yield
        ps = bcast_sumsq(o_sb)
        rs = bigB.next()
        P.act(rs[:, 0:512], ps[:, :], AF.Ln, [ps], [rs], bias=C.eps_t[:, 0:1], scale=1.0 / 128)
        P.act(rs[:, 0:512], rs[:, 0:512], AF.Exp, [rs], [rs], scale=-0.5)
        yield
        P.stt('dve', rs[:, 0:512], o_sb[:, :], cv[:, o_nw:o_nw + 1], rs[:, 0:512], ALU.mult, ALU.mult, [o_sb, cv, rs], [rs])
        y_bf = role("y_bf", [128, 512], BF16)
        P.tt('dve', y_bf[:, :], rs[:, 0:512], sz[:, :], ALU.mult, [rs, sz], [y_bf])
        P.dma('pool', yT.t.ap()[h * 128:(h + 1) * 128, sc.tok], y_bf[:, :], [y_bf], [G.y_dep[sc.seg]])
        yield

    done = []

    def bulk_gen():
        for seg in range(NSEG):
            sc = seg_prologue(seg)
            yield
            for h in range(NH):
                a = {}
                yield from stageA(sc, h, a)
                done.append((a, sc))
                yield
                yield from stageL(sc, h)

    bulk = bulk_gen()
    state = {'alive': True}

    def pump():
        if state['alive']:
            try:
                next(bulk)
            except StopIteration:
                state['alive'] = False

    idx = 0
    RATIO = 1
    for seg in range(NSEG):
        for h in range(NH):
            while len(done) <= idx:
                pump()
            a, sc = done[idx]
            idx += 1
            tick = 0
            for _ in stageB(sc, h, a):
                tick += 1
                for _k in range(2):
                    if len(done) <= idx:
                        pump()
    while state['alive']:
        pump()
    P.end_phase()


def me_host_inputs(NH, ev_w_in, dn_conv_w, dn_a_log, dn_dt_bias, dn_norm_w, lru_conv_w, lru_conv_b, lru_wa, lru_ba,
                   lru_wx, lru_bx, lru_lambda):
    cols = []
    for h in range(NH):
        for base in (0, 1024, 2048, 3072):
            cols.append(np.arange(base + 128 * h, base + 128 * h + 128))
    for g in range(NH):
        cols.append(np.arange(4112 + 128 * g, 4112 + 128 * g + 128))
        cols.append(np.arange(5136 + 128 * g, 5136 + 128 * g + 128))
    w_in = tile_w(ev_w_in[:, np.concatenate(cols)])
    rep = []
    for h in range(NH):
        rep.append(np.full(128, 4096 + h))
        rep.append(np.full(128, 4104 + h))
    w_ab_rep = tile_w(ev_w_in[:, np.concatenate(rep)])
    abcols = [4096 + h for h in range(NH)] + [4104 + h for h in range(NH)]
    w_ab_tok = np.ascontiguousarray(ev_w_in[:, abcols].reshape(16, 128, 2 * NH).transpose(1, 0, 2))
    cv = np.zeros((128, 22 * NH + 1), np.float32)
    for h in range(NH):
        for xi, base in enumerate((0, 1024, 2048)):
            t = 3 * h + xi
            cv[:, 4 * t:4 * t + 4] = dn_conv_w[:, base + 128 * h:base + 128 * h + 128].T
    for g in range(NH):
        sl = slice(128 * g, 128 * g + 128)
        cv[:, 12 * NH + 4 * g:12 * NH + 4 * g + 4] = lru_conv_w[:, sl].T
        cv[:, 16 * NH + g] = lru_conv_b[sl]
        cv[:, 17 * NH + g] = lru_ba[sl]
        cv[:, 18 * NH + g] = lru_bx[sl]
        cv[:, 19 * NH + g] = lru_lambda[sl]
    cv[:, 20 * NH] = dn_norm_w
    for h in range(NH):
        cv[:, 20 * NH + 1 + h] = dn_a_log[h]
        cv[:, 21 * NH + 1 + h] = dn_dt_bias[h]
    lw = np.ascontiguousarray(np.concatenate([lru_wa[:NH], lru_wx[:NH]], axis=0))
    return {"w_in": w_in, "w_ab_rep": w_ab_rep, "w_ab_tok": w_ab_tok, "cvec": cv, "lru_w": lw}


SWA_BR = ((128, 1), (512, 4), (2048, 16))
NEG = -30000.0
RC = 128


def ss(start, n, step):
    return slice(start, start + (n - 1) * step + 1, step)


def mo_consts():
    iq = np.arange(128)[None, :]
    ik = np.arange(128)[:, None]
    tabs = [np.eye(128)]
    for (_, d) in SWA_BR:
        tabs += [np.where(ik >= iq, (128 + iq - ik) * float(d), 0.0), np.where(ik <= iq, (iq - ik) * float(d), 0.0)]
    tabs += [np.where(ik >= iq, 0.0, NEG), np.where(ik <= iq, 0.0, NEG)]
    for r in range(NHR_O):
        lg = np.log1p(-2.0 ** (-5.0 - r))
        rel = iq - ik
        dm = np.where(rel >= 0, np.exp(np.maximum(rel, 0) * lg), 0.0) * (128.0 ** -0.5)
        qdec = np.broadcast_to(np.exp((np.arange(128) + 1.0) * lg)[None, :], (128, 128))
        tabs += [dm, qdec]
    tab = np.concatenate(tabs, axis=1).astype(np.float32)
    cols = np.zeros((128, NHS_O + 2 * NHR_O), np.float32)
    for h in range(NHS_O):
        cols[:, h] = -(2.0 ** (-(h + 1)))
    for r in range(NHR_O):
        lg = np.log1p(-2.0 ** (-5.0 - r))
        cols[:, NHS_O + r] = np.exp((127.0 - np.arange(128)) * lg) * (128.0 ** -0.5)
        cols[:, NHS_O + NHR_O + r] = np.exp(128.0 * lg)
    return np.ascontiguousarray(tab), cols


def mo_host_inputs(od_w_in):
    cols = []
    for h in range(NHS_O):
        for base in (0, 1024, 2048):
            cols.append(np.arange(base + 128 * h, base + 128 * h + 128))
    for r in range(NHR_O):
        cols.append(np.arange(3072 + 128 * r, 3072 + 128 * r + 128))
        cols.append(np.arange(3584 + 128 * r, 3584 + 128 * r + 128))
        cols.append(np.arange(4096 + 256 * r, 4096 + 256 * r + 256))
        cols.append(np.arange(5120 + 256 * r, 5120 + 256 * r + 256))
    return {"w_in": tile_w(od_w_in[:, np.concatenate(cols)])}


def emit_MO(P, G, j, T, NHS, NHR, bg):
    P.begin_phase()
    C = Ctx()
    NBLK = T // 512
    NSB = T // 2048
    hnT, yT = G.hnT, G.yT
    w_in = G.o_w_in.t.ap()[j]
    NTAB = 9 + 2 * NHR
    ctab_d = G.o_ctab.t.ap()
    ccol_d = G.o_ccol.t.ap()
    load_consts(P, C)
    ctab = P.sb("ctab_sb", [128, NTAB * 128])
    P.dma('sp', ctab[:, :], ctab_d, [], [ctab])
    ccol = P.sb("ccol_sb", [128, NHS + 2 * NHR])
    P.dma('sp', ccol[:, :], ccol_d, [], [ccol])
    htab = P.sb("htab", [128, 6 * 128])
    ident_bf = P.sb("ident_bf", [128, 128], BF16)
    P.cp('dve', ident_bf[:, :], ctab[:, 0:128], [ctab], [ident_bf])
    ones_f = P.sb("ones_f", [128, 128])
    P.memset('dve', ones_f[:, :], 1.0, [ones_f])

    def btab(br, which):
        k = br * 2 + which
        return htab[:, k * 128:(k + 1) * 128]

    pmm = Pool([P.ps("pmm%d" % i, [128, 512]) for i in range(1)])
    pss = Pool([P.ps("pss%d" % i, [128, 512]) for i in range(2)])
    psn = Pool([P.ps("psn%d" % i, [128, 512]) for i in range(2)])
    psd = Pool([P.ps("psd%d" % i, [128, 512]) for i in range(2)])
    ptb = Pool([P.ps("ptb%d" % i, [128, 1024], BF16) for i in range(1)])
    hnp = P.sbpool("hn", [128, KC, 512], BF16, 2)
    w16 = P.sbpool("w16", [128, 16, 128], BF16, 6)
    pools = {}

    def role(name, shape, dtype, n=2):
        if name not in pools:
            pools[name] = P.sbpool(name, shape, dtype, n)
        return pools[name].next()

    qT = P.sb("qT", [128, T], BF16)
    kT = P.sb("kT", [128, T], BF16)
    vT = P.sb("vT", [128, T], BF16)
    for h in range(NHS):
        for br in range(3):
            for which in range(2):
                k = br * 2 + which
                P.stt('dve', htab[:, k * 128:(k + 1) * 128], ctab[:, (1 + k) * 128:(2 + k) * 128], ccol[:, h:h + 1],
                      ctab[:, (7 + which) * 128:(8 + which) * 128], ALU.mult, ALU.add, [ctab, ccol], [htab])
        ws = []
        for x in range(3):
            w = w16.next()
            P.dma('pool', w[:, :, :], w_in[3 * h + x], [], [w])
            ws.append(w)
        for blk in range(NBLK):
            tok = slice(blk * 512, (blk + 1) * 512)
            hn = hnp.next()
            P.dma('sp', hn[:, :, :], fm(hnT.t.ap())[:, :, tok], [G.hn_dep[blk]], [hn])
            bg()
            for x, dst in enumerate((qT, kT, vT)):
                ps = pmm.next()
                for kc in range(KC):
                    P.mm(ps[:, :], ws[x][:, kc, :], hn[:, kc, :], kc == 0, kc == KC - 1, [ws[x], hn], [ps])
                P.cp('act', dst[:, tok], ps[:, :], [ps], [dst])
        for sb_ in range(NSB):
            base = sb_ * 2048
            accN = role("accN", [128, 2048], F32, 1)
            accD = role("accD", [128, 2048], F32, 1)
            qblocks = []
            for br, (_, d) in enumerate(SWA_BR):
                nloc = 2048 // (128 * d)
                for r in range(d):
                    for nl in range(nloc):
                        qblocks.append((br, d, r, sb_ * nloc + nl, nl == 0))
            vcache = {}

            def vblock(d, r, nbk):
                key = (d, r, nbk)
                if key in vcache:
                    return vcache[key]
                ks = ss(nbk * 128 * d + r, 128, d)
                pt = ptb.next()
                P.tr(pt[:, 0:128], vT[:, ks], ident_bf[:, :], [vT, ident_bf], [pt])
                vb = role("vb", [128, 128], BF16, 8)
                P.cp('act', vb[:, :], pt[:, 0:128], [pt], [vb])
                vcache.clear()
                vcache[key] = vb
                return vb

            def stage1(qb):
                br, d, r, nb, first = qb
                qs = ss(nb * 128 * d + r, 128, d)
                c = Ctx()
                c.qb = qb
                c.vprev = vblock(d, r, nb - 1) if nb > 0 else None
                c.vcur = vblock(d, r, nb)
                lo = 0 if nb > 0 else 128
                ps = pss.next()
                if nb > 0:
                    P.mm(ps[:, 0:128], kT[:, ss((nb - 1) * 128 * d + r, 128, d)], qT[:, qs], True, True, [kT, qT], [ps])
                P.mm(ps[:, 128:256], kT[:, qs], qT[:, qs], True, True, [kT, qT], [ps])
                tb = role("tb", [128, 256], F32, 3)
                P.stt('dve', tb[:, lo:256], ps[:, lo:256], 128.0 ** -0.5, htab[:, br * 256 + lo:br * 256 + 256], ALU.mult, ALU.add,
                      [ps, htab], [tb])
                c.pT = role("pT", [128, 256], BF16, 3)
                P.act(c.pT[:, lo:256], tb[:, lo:256], AF.Exp, [tb], [c.pT])
                return c

            def stage2(c):
                br, d, r, nb, first = c.qb
                pn = psn.next()
                pd = psd.next()
                if nb > 0:
                    P.mm(pn[:, 0:128], c.vprev[:, :], c.pT[:, 0:128], True, False, [c.vprev, c.pT], [pn])
                    P.mm(pd[:, 0:128], C.ones_bf[:, :], c.pT[:, 0:128], True, False, [C.ones_bf, c.pT], [pd])
                P.mm(pn[:, 0:128], c.vcur[:, :], c.pT[:, 128:256], nb == 0, True, [c.vcur, c.pT], [pn])
                P.mm(pd[:, 0:128], C.ones_bf[:, :], c.pT[:, 128:256], nb == 0, True, [C.ones_bf, c.pT], [pd])
                ql = ss(nb * 128 * d + r - base, 128, d)
                if br == 0:
                    P.cp('act', accN[:, ql], pn[:, 0:128], [pn], [accN])
                    P.cp('dve', accD[:, ql], pd[:, 0:128], [pd], [accD])
                else:
                    P.tt('dve', accN[:, ql], accN[:, ql], pn[:, 0:128], ALU.add, [accN, pn], [accN])
                    P.tt('dve', accD[:, ql], accD[:, ql], pd[:, 0:128], ALU.add, [accD, pd], [accD])

            pend = None
            for qb in qblocks:
                cur = stage1(qb)
                if pend is not None:
                    stage2(pend)
                pend = cur
            stage2(pend)
            P.recip(accD[:, :], accD[:, :], [accD], [accD])
            ysw = role("ysw", [128, 2048], BF16, 1)
            P.tt('dve', ysw[:, :], accN[:, :], accD[:, :], ALU.mult, [accN, accD], [ysw])
            P.dma('sp', yT.t.ap()[h * 128:(h + 1) * 128, base:base + 2048], ysw[:, :], [ysw], [G.y_dep[4 * sb_ + q_] for q_ in range(4)])

    St = [P.sb("St%d" % r, [128, 256]) for r in range(NHR)]
    Stb = [P.sb("Stb%d" % r, [128, 256], BF16) for r in range(NHR)]
    for r in range(NHR):
        P.memset('dve', St[r][:, :], 0.0, [St[r]])
        P.memset('dve', Stb[r][:, :], 0.0, [Stb[r]])
    for r in range(NHR):
        dmT = ctab[:, (9 + 2 * r) * 128:(10 + 2 * r) * 128]
        qdec = ctab[:, (10 + 2 * r) * 128:(11 + 2 * r) * 128]
        ws = []
        for x in range(6):
            w = w16.next()
            P.dma('pool', w[:, :, :], w_in[3 * NHS + 6 * r + x], [], [w])
            ws.append(w)
        def rstageA(blk, outs):
            tok = slice(blk * 512, (blk + 1) * 512)
            hn = hnp.next()
            P.dma('sp', hn[:, :, :], fm(hnT.t.ap())[:, :, tok], [G.hn_dep[blk]], [hn])
            bg()
            for x in range(6):
                ps = pmm.next()
                for kc in range(KC):
                    P.mm(ps[:, :], ws[x][:, kc, :], hn[:, kc, :], kc == 0, kc == KC - 1, [ws[x], hn], [ps])
                    if kc % 4 == 3 and kc != KC - 1:
                        yield
                if x < 4:
                    o = role("rp%d" % x, [128, 512], BF16)
                    P.cp('act', o[:, :], ps[:, :], [ps], [o])
                else:
                    o = role("rp%d" % x, [128, 512], F32)
                    P.act(o[:, :], ps[:, :], AF.Silu, [ps], [o])
                outs.append(o)
                yield
        def rstageB(blk, outs):
            tok = slice(blk * 512, (blk + 1) * 512)
            rq, rk, rv0, rv1, sg0, sg1 = outs
            o_sb = [role("ro0", [128, 512], F32), role("ro1", [128, 512], F32)]
            for c in range(4):
                tp = slice(c * RC, (c + 1) * RC)
                ps = pss.next()
                P.mm(ps[:, 0:128], rk[:, tp], rq[:, tp], True, True, [rk, rq], [ps])
                pT = role("rpT", [128, 128], BF16)
                P.tt('dve', pT[:, :], ps[:, 0:128], dmT, ALU.mult, [ps, ctab], [pT])
                pt = ptb.next()
                P.tr(pt[:, 0:128], rv0[:, tp], ident_bf[:, :], [rv0, ident_bf], [pt])
                P.tr(pt[:, 128:256], rv1[:, tp], ident_bf[:, :], [rv1, ident_bf], [pt])
                P.tr(pt[:, 256:384], rk[:, tp], ident_bf[:, :], [rk, ident_bf], [pt])
                Vt = role("rVt", [128, 256], BF16)
                P.cp('act', Vt[:, :], pt[:, 0:256], [pt], [Vt])
                kd = role("rkd", [128, 128], BF16)
                P.ts('dve', kd[:, :], pt[:, 256:384], ccol[:, NHS + r:NHS + r + 1], ALU.mult, [pt, ccol], [kd])
                qd = role("rqd", [128, 128], BF16)
                P.tt('dve', qd[:, :], rq[:, tp], qdec, ALU.mult, [rq, ctab], [qd])
                yield
                for dvt in range(2):
                    po = psn.next()
                    P.mm(po[:, 0:128], Vt[:, dvt * 128:(dvt + 1) * 128], pT[:, :], True, False, [Vt, pT], [po])
                    P.mm(po[:, 0:128], Stb[r][:, dvt * 128:(dvt + 1) * 128], qd[:, :], False, True, [Stb[r], qd], [po])
                    P.cp('act', o_sb[dvt][:, tp], po[:, 0:128], [po], [o_sb[dvt]])
                    yield
                pk = psd.next()
                P.mm(pk[:, 0:256], kd[:, :], Vt[:, :], True, True, [kd, Vt], [pk])
                P.stt('dve', Stb[r][:, :], St[r][:, :], ccol[:, NHS + NHR + r:NHS + NHR + r + 1], pk[:, 0:256], ALU.mult, ALU.add,
                      [St[r], ccol, pk], [Stb[r]])
                P.stt('dve', St[r][:, :], St[r][:, :], ccol[:, NHS + NHR + r:NHS + NHR + r + 1], pk[:, 0:256], ALU.mult, ALU.add,
                      [St[r], ccol, pk], [St[r]])
                yield
            p1 = pss.next()
            P.mm(p1[:, :], ones_f[:, :], o_sb[0][:, :], True, False, [ones_f, o_sb[0]], [p1])
            P.mm(p1[:, :], ones_f[:, :], o_sb[1][:, :], False, True, [ones_f, o_sb[1]], [p1])
            mean = role("rmean", [128, 512], F32)
            P.act(mean[:, :], p1[:, :], AF.Copy, [p1], [mean], scale=1.0 / 256)
            yield
            p2 = pss.next()
            for dvt in range(2):
                sq = role("rsq", [128, 512], F32)
                P.act(sq[:, :], o_sb[dvt][:, :], AF.Square, [o_sb[dvt]], [sq])
                P.mm(p2[:, :], ones_f[:, :], sq[:, :], dvt == 0, dvt == 1, [ones_f, sq], [p2])
            msq = role("rmsq", [128, 512], F32)
            P.tt('dve', msq[:, :], mean[:, :], mean[:, :], ALU.mult, [mean], [msq])
            P.stt('dve', msq[:, :], p2[:, :], 1.0 / 256, msq[:, :], ALU.mult, ALU.subtract, [p2, msq], [msq])
            P.act(msq[:, :], msq[:, :], AF.Sqrt, [msq], [msq], bias=C.eps_t[:, 1:2])
            P.recip(msq[:, :], msq[:, :], [msq], [msq])
            yield
            for dvt, sg in enumerate((sg0, sg1)):
                t = role("rt", [128, 512], F32)
                P.tt('dve', t[:, :], o_sb[dvt][:, :], mean[:, :], ALU.subtract, [o_sb[dvt], mean], [t])
                P.tt('dve', t[:, :], t[:, :], msq[:, :], ALU.mult, [t, msq], [t])
                yb = role("ryb", [128, 512], BF16)
                P.tt('dve', yb[:, :], t[:, :], sg[:, :], ALU.mult, [t, sg], [yb])
                row = 128 * NHS + r * 256 + dvt * 128
                P.dma('sp', yT.t.ap()[row:row + 128, tok], yb[:, :], [yb], [G.y_dep[blk]])
                yield
        rdone = []

        def rbulk():
            for blk in range(NBLK):
                outs = []
                yield from rstageA(blk, outs)
                rdone.append(outs)
                yield

        rb = rbulk()
        ralive = [True]

        def rpump():
            if ralive[0]:
                try:
                    next(rb)
                except StopIteration:
                    ralive[0] = False

        for blk in range(NBLK):
            while len(rdone) <= blk:
                rpump()
            tick = 0
            for _ in rstageB(blk, rdone[blk]):
                tick += 1
                if tick % 2 == 0 and len(rdone) <= blk + 1:
                    rpump()
        while ralive[0]:
            rpump()
    P.end_phase()


NH_E = 8
NHS_O, NHR_O = 8, 4


def build_fused(T, depth):
    nc = bass.Bass("TRN2", target_bir_lowering=False)
    P = Prog(nc)
    G = Ctx()
    n_ev, n_od = (depth + 1) // 2, depth // 2
    NB = T // 512
    ext = lambda name, shape, dt=F32: P.dram(name, shape, dt, kind="ExternalInput")
    G.xT = ext("xT", [D_MODEL, T])
    G.pT = ext("pT", [depth, 256, T])
    G.gains = ext("gains", [128, KC + 3 * KC * depth])
    G.wf = {}
    G.wbf = {}
    G.wdep = {}
    for name, nt, kc in TW:
        G.wf[name] = ext(name, [depth, nt, 128, kc, 128])
        G.wbf[name] = P.dram(name + "_bf", [depth, nt, 128, kc, 128], BF16)
        for i in range(depth):
            for j in range(nt):
                G.wdep[(name, i, j)] = Dep()
    G.e_w_in = ext("e_w_in", [n_ev, 6 * NH_E, 128, 16, 128])
    G.e_w_ab_rep = ext("e_w_ab_rep", [n_ev, 2 * NH_E, 128, 16, 128])
    G.e_w_ab_tok = ext("e_w_ab_tok", [n_ev, 128, 16, 2 * NH_E])
    G.e_cvec = ext("e_cvec", [n_ev, 128, 22 * NH_E + 1])
    G.e_lru_w = ext("e_lru_w", [n_ev, 2 * NH_E, 128, 128])
    G.mwbf = {"e_w_in": P.dram("e_w_in_bf", [n_ev, 6 * NH_E, 128, 16, 128], BF16),
              "e_w_ab_rep": P.dram("e_w_ab_rep_bf", [n_ev, 2 * NH_E, 128, 16, 128], BF16)}
    G.mwf = {"e_w_in": G.e_w_in, "e_w_ab_rep": G.e_w_ab_rep}
    G.mwdep = {}
    for jj in range(n_ev):
        for t in range(6 * NH_E):
            G.mwdep[("e_w_in", jj, t)] = Dep()
        for t in range(2 * NH_E):
            G.mwdep[("e_w_ab_rep", jj, t)] = Dep()
    G.e_ctab = ext("e_ctab", [128, 7 * 128 + 512])
    G.o_w_in = ext("o_w_in", [max(n_od, 1), 3 * NHS_O + 6 * NHR_O, 128, 16, 128])
    G.o_ctab = ext("o_ctab", [128, (9 + 2 * NHR_O) * 128])
    G.o_ccol = ext("o_ccol", [128, NHS_O + 2 * NHR_O])
    G.outT = P.dram("outT", [D_MODEL, T], F32, kind="ExternalOutput")
    G.hT = P.dram("hT_s", [D_MODEL, T], F32)
    G.hnT = P.dram("hnT_s", [D_MODEL, T], BF16)
    G.yT = P.dram("yT_s", [D_MODEL, T], BF16)
    G.h_dep = [Dep() for _ in range(NB)]
    G.hn_dep = [Dep() for _ in range(NB)]
    G.y_dep = [Dep() for _ in range(NB)]

    def make_bg(i, ncalls):
        todo = [(name, j) for name, nt, kc in TW for j in range(nt)]
        per = -(-len(todo) // ncalls)

        def bg():
            for _ in range(per):
                if todo:
                    name, j = todo.pop(0)
                    P.dma('pool', G.wbf[name].t.ap()[i, j], G.wf[name].t.ap()[i, j], [], [G.wdep[(name, i, j)]])
        return bg, todo

    def make_mbg(jj, ncalls):
        todo = []
        for h in range(NH_E):
            todo += [("e_w_in", 4 * h + x) for x in range(4)] + [("e_w_ab_rep", 2 * h), ("e_w_ab_rep", 2 * h + 1)]
            todo += [("e_w_in", 4 * NH_E + 2 * h), ("e_w_in", 4 * NH_E + 2 * h + 1)]
        per = -(-len(todo) // ncalls)

        def bgm():
            for _ in range(per):
                if todo:
                    name, t = todo.pop(0)
                    P.dma('pool', G.mwbf[name].t.ap()[jj, t], G.mwf[name].t.ap()[jj, t], [], [G.mwdep[(name, jj, t)]])
        return bgm, todo

    bgm, mtodo = make_mbg(0, 4)
    emit_N(P, G, T, bgm)
    while mtodo:
        bgm()
    for i in range(depth):
        if i % 2 == 0:
            bg, todo = make_bg(i, NB)
            emit_ME(P, G, i // 2, T, NH_E, bg)
        else:
            bg, todo = make_bg(i, NB * (NHS_O + NHR_O))
            emit_MO(P, G, i // 2, T, NHS_O, NHR_O, bg)
        while todo:
            bg()
        if i % 2 == 1 and i + 1 < depth:
            bgm, mtodo = make_mbg((i + 1) // 2, NB)
            emit_T(P, G, i, T, i == depth - 1, bgm)
            while mtodo:
                bgm()
        else:
            emit_T(P, G, i, T, i == depth - 1)
    P.fence('sp')
    P.emit()
    return nc


_NC_CACHE = {}


def host_inputs(inputs, depth):
    n_ev, n_od = (depth + 1) // 2, depth // 2
    gl = [col_vec(inputs["ln_mix_w"][0])]
    for i in range(depth):
        g_next = inputs["ln_final_w"] if i == depth - 1 else inputs["ln_mix_w"][i + 1]
        gl += [col_vec(inputs["ln_mlp_w"][i]), col_vec(inputs["ln_ple_w"][i]), col_vec(g_next)]
    H = {"gains": np.ascontiguousarray(np.concatenate(gl, axis=1))}
    wout = [inputs["ev_w_out"][i // 2] if i % 2 == 0 else inputs["od_w_out"][i // 2] for i in range(depth)]
    H["w_out"] = np.stack([tile_w(w) for w in wout])
    H["w_up"] = np.stack([tile_w(inputs["w_up"][i]) for i in range(depth)])
    H["w_dn"] = np.stack([tile_w(inputs["w_down"][i]) for i in range(depth)])
    H["w_g"] = np.stack([tile_w(inputs["w_ple_gate"][i]) for i in range(depth)])
    H["w_p"] = np.stack([tile_w(inputs["w_ple_proj"][i]) for i in range(depth)])
    names = ["ev_w_in", "dn_conv_w", "dn_a_log", "dn_dt_bias", "dn_norm_w", "lru_conv_w", "lru_conv_b", "lru_wa", "lru_ba",
             "lru_wx", "lru_bx", "lru_lambda"]
    ev = [me_host_inputs(NH_E, *[inputs[n][j] for n in names]) for j in range(n_ev)]
    for k, dst in (("w_in", "e_w_in"), ("w_ab_rep", "e_w_ab_rep"), ("w_ab_tok", "e_w_ab_tok"), ("cvec", "e_cvec"),
                   ("lru_w", "e_lru_w")):
        H[dst] = np.stack([e[k] for e in ev])
    H["e_ctab"] = me_consts()
    od = [mo_host_inputs(inputs["od_w_in"][j]) for j in range(n_od)]
    H["o_w_in"] = np.stack([o["w_in"] for o in od]) if od else np.zeros((1, 3 * NHS_O + 6 * NHR_O, 128, 16, 128), np.float32)
    H["o_ctab"], H["o_ccol"] = mo_consts()
    return H


def run_fused(inputs):
    x = inputs["x"]
    B, T, D = x.shape
    depth = inputs["w_up"].shape[0]
    key = (T, depth)
    if key not in _NC_CACHE:
        _NC_CACHE[key] = build_fused(T, depth)
    nc = _NC_CACHE[key]
    H = host_inputs(inputs, depth)
    real = []
    for b in range(B):
        m = dict(H)
        m["xT"] = np.ascontiguousarray(x[b].T)
        m["pT"] = np.ascontiguousarray(np.transpose(inputs["p"][:, b], (0, 2, 1)))
        real.append(m)
    if B == 4:
        slots = [0, 1, 4, 5]
        zero = {k: np.zeros_like(v) for k, v in real[0].items()}
        maps = [zero] * 6
        maps = list(maps)
        for b, c in enumerate(slots):
            maps[c] = real[b]
    else:
        slots = list(range(B))
        maps = real
    res = run_bass_kernel_spmd(nc, maps, core_ids=list(range(len(maps))))
    out = np.empty((B, T, D), np.float32)
    for b in range(B):
        out[b] = res.results[slots[b]]["outT"].T
    return out


def kernel(x, p, ln_mix_w, ln_mlp_w, ln_ple_w, w_up, w_down, w_ple_proj, w_ple_gate, ln_final_w,
           ev_w_in, ev_w_out, dn_conv_w, dn_a_log, dn_dt_bias, dn_norm_w,
           lru_conv_w, lru_conv_b, lru_wa, lru_ba, lru_wx, lru_bx, lru_lambda, od_w_in, od_w_out):
    inputs = {k: np.asarray(v) for k, v in locals().items()}
    return run_fused(inputs)
```
